# Optimizing a Trainium2 kernel written in Bass

```python
import math
import jax, jax.numpy as jnp
from jax import lax
import numpy as np

D_MODEL = 1024
BATCH = 8
SEQ = 8192
DEPTH = 4

D_MIX = D_MODEL
GM_HEADS = 4
GM_HEAD_DIM = D_MODEL // 16
GM_WIDTH = GM_HEADS * GM_HEAD_DIM
GM_CHUNK = 128
MLA_HEADS = 8
MLA_NOPE = 64
MLA_ROPE = 32
MLA_V = 64
MLA_WIDTH = MLA_HEADS * MLA_V
Q_LORA = 256
KV_LORA = 128
ROPE_BASE = 10000.0
Q_BLOCK = 128
SSM_GROUPS = 16
SSM_GROUP_CH = 16
SSM_WIDTH = SSM_GROUPS * SSM_GROUP_CH
SSM_STATE = 64
DT_MIN = 1e-3
DT_MAX = 1e-1
IN_COLS = 2 * GM_WIDTH + Q_LORA + KV_LORA + MLA_ROPE + SSM_WIDTH
D_FF = 2816
ALPHA = (2 * DEPTH) ** 0.25
BETA = (8 * DEPTH) ** -0.25
LN_EPS = 1e-5
RMS_EPS = 1e-6
NEG_BIG = -1e30

kernel_name = "hybrid_gmlp_mla_s5_macaron_deepnorm"


def layer_norm(x, g, b):
    xf = x.astype(jnp.float32)
    mu = jnp.mean(xf, axis=-1, keepdims=True)
    var = jnp.mean(jnp.square(xf - mu), axis=-1, keepdims=True)
    y = (xf - mu) * lax.rsqrt(var + LN_EPS) * g.astype(jnp.float32) + b.astype(jnp.float32)
    return y.astype(x.dtype)


def rms_norm(x, g):
    xf = x.astype(jnp.float32)
    y = xf * lax.rsqrt(jnp.mean(jnp.square(xf), axis=-1, keepdims=True) + RMS_EPS) * g.astype(jnp.float32)
    return y.astype(x.dtype)


def rms_only(x):
    xf = x.astype(jnp.float32)
    return (xf * lax.rsqrt(jnp.mean(jnp.square(xf), axis=-1, keepdims=True) + RMS_EPS)).astype(x.dtype)


def swiglu(x, w_gate, w_up, w_down):
    return (jax.nn.silu(x @ w_gate) * (x @ w_up)) @ w_down


def rope(x, cos, sin):
    half = x.shape[-1] // 2
    x1, x2 = x[..., :half], x[..., half:]
    cos = cos.astype(x.dtype)
    sin = sin.astype(x.dtype)
    return jnp.concatenate([x1 * cos - x2 * sin, x2 * cos + x1 * sin], axis=-1)


def gmlp_mixer(u, v, norm_g, ws, bs):
    b, s, _ = u.shape
    u = jax.nn.gelu(u)
    v = jax.nn.gelu(v).reshape(b, s, GM_HEADS, GM_HEAD_DIM)
    vf = v.astype(jnp.float32)
    mu = jnp.mean(vf, axis=-1, keepdims=True)
    var = jnp.mean(jnp.square(vf - mu), axis=-1, keepdims=True)
    v = ((vf - mu) * lax.rsqrt(var + LN_EPS) * norm_g.reshape(GM_HEADS, GM_HEAD_DIM).astype(jnp.float32)).astype(u.dtype)
    v = v.reshape(b, s // GM_CHUNK, GM_CHUNK, GM_HEADS, GM_HEAD_DIM)
    mask = jnp.tril(jnp.ones((GM_CHUNK, GM_CHUNK), dtype=ws.dtype))
    w_causal = ws * mask[None]
    z = jnp.einsum("bnchd,htc->bnthd", v, w_causal) + bs.T[:, :, None]
    return u * z.reshape(b, s, GM_WIDTH)


def mla_mixer(c_q, c_kv, k_rope_in, cos, sin, q_norm_g, w_uq, kv_norm_g, w_ukv):
    b, s, _ = c_q.shape
    q = (rms_norm(c_q, q_norm_g) @ w_uq).reshape(b, s, MLA_HEADS, MLA_NOPE + MLA_ROPE)
    q_nope = q[..., :MLA_NOPE]
    q_rope = rope(q[..., MLA_NOPE:], cos[:, :, None, :], sin[:, :, None, :])
    kv = (rms_norm(c_kv, kv_norm_g) @ w_ukv).reshape(b, s, MLA_HEADS, MLA_NOPE + MLA_V)
    k_nope = kv[..., :MLA_NOPE]
    v = kv[..., MLA_NOPE:]
    k_rope = rope(k_rope_in, cos, sin)
    nblk = s // Q_BLOCK
    qn_b = q_nope.reshape(b, nblk, Q_BLOCK, MLA_HEADS, MLA_NOPE).transpose(1, 0, 2, 3, 4)
    qr_b = q_rope.reshape(b, nblk, Q_BLOCK, MLA_HEADS, MLA_ROPE).transpose(1, 0, 2, 3, 4)
    scale = (MLA_NOPE + MLA_ROPE) ** -0.5
    kpos = jnp.arange(s)

    def block(args):
        qn, qr, i = args
        sc = jnp.einsum("bqhd,bkhd->bhqk", qn, k_nope) + jnp.einsum("bqhr,bkr->bhqk", qr, k_rope)
        sc = sc.astype(jnp.float32) * scale
        qpos = i * Q_BLOCK + jnp.arange(Q_BLOCK)
        causal = kpos[None, :] <= qpos[:, None]
        sc = jnp.where(causal[None, None], sc, NEG_BIG)
        p = jax.nn.softmax(sc, axis=-1).astype(v.dtype)
        return jnp.einsum("bhqk,bkhd->bqhd", p, v)

    o = lax.map(block, (qn_b, qr_b, jnp.arange(nblk)))
    return o.transpose(1, 0, 2, 3, 4).reshape(b, s, MLA_WIDTH)


def _complex_scan_combine(e1, e2):
    a1r, a1i, b1r, b1i = e1
    a2r, a2i, b2r, b2i = e2
    ar = a1r * a2r - a1i * a2i
    ai = a1r * a2i + a1i * a2r
    br = a2r * b1r - a2i * b1i + b2r
    bi = a2r * b1i + a2i * b1r + b2i
    return (ar, ai, br, bi)


def s5_mixer(u, a_re, a_im, b_re, b_im, c_re, c_im, d_skip, log_dt, glu_w, glu_b):
    bsz, s, _ = u.shape
    f32 = jnp.float32
    uf = u.astype(f32).reshape(bsz, s, SSM_GROUPS, SSM_GROUP_CH)
    ar, ai = a_re.astype(f32), a_im.astype(f32)
    dt = jnp.exp(log_dt.astype(f32))[:, None]
    mag = jnp.exp(ar * dt)
    abr = mag * jnp.cos(ai * dt)
    abi = mag * jnp.sin(ai * dt)
    den = ar * ar + ai * ai
    cr = ((abr - 1.0) * ar + abi * ai) / den
    ci = (abi * ar - (abr - 1.0) * ai) / den
    br, bi = b_re.astype(f32), b_im.astype(f32)
    bbr = cr[..., None] * br - ci[..., None] * bi
    bbi = cr[..., None] * bi + ci[..., None] * br
    bu_re = jnp.einsum("bsgc,gpc->bsgp", uf, bbr)
    bu_im = jnp.einsum("bsgc,gpc->bsgp", uf, bbi)
    a_seq_re = jnp.broadcast_to(abr[None, None], (1, s, SSM_GROUPS, SSM_STATE))
    a_seq_im = jnp.broadcast_to(abi[None, None], (1, s, SSM_GROUPS, SSM_STATE))
    _, _, h_re, h_im = lax.associative_scan(_complex_scan_combine, (a_seq_re, a_seq_im, bu_re, bu_im), axis=1)
    y = (jnp.einsum("bsgp,gcp->bsgc", h_re, c_re.astype(f32))
         - jnp.einsum("bsgp,gcp->bsgc", h_im, c_im.astype(f32))
         + d_skip.astype(f32) * uf)
    y = jax.nn.gelu(y).reshape(bsz, s, SSM_WIDTH)
    y = y * jax.nn.sigmoid(y @ glu_w.astype(f32) + glu_b.astype(f32))
    return y.astype(u.dtype)


def setup_inputs(seed: int = 0) -> dict:
    key = jax.random.key(seed)
    ks = jax.random.split(key, 32)
    f32 = jnp.float32
    L, D, F = DEPTH, D_MODEL, D_FF

    def nrm(k, shape, scale):
        return jax.random.normal(k, shape, f32) * scale

    x = jax.random.normal(ks[0], (BATCH, SEQ, D), f32)
    offset = jax.random.randint(ks[1], (BATCH, 1), 0, 1024, dtype=jnp.int32)
    positions = (offset + jnp.arange(SEQ, dtype=jnp.int32)[None, :]).astype(jnp.int32)
    ln_g = 1.0 + nrm(ks[2], (L, 3, D), 0.01)
    ln_b = nrm(ks[3], (L, 3, D), 0.01)
    ffn1_w_gate = nrm(ks[4], (L, D, F), D ** -0.5)
    ffn1_w_up = nrm(ks[5], (L, D, F), D ** -0.5)
    ffn1_w_down = nrm(ks[6], (L, F, D), BETA * F ** -0.5)
    w_in = nrm(ks[7], (L, D, IN_COLS), D ** -0.5)
    gmlp_norm_g = 1.0 + nrm(ks[8], (L, GM_WIDTH), 0.01)
    gmlp_ws = nrm(ks[9], (L, GM_HEADS, GM_CHUNK, GM_CHUNK), 0.5 * GM_CHUNK ** -0.5)
    gmlp_bs = 1.0 + nrm(ks[10], (L, GM_HEADS, GM_CHUNK), 0.01)
    mla_q_norm_g = 1.0 + nrm(ks[11], (L, Q_LORA), 0.01)
    mla_w_uq = nrm(ks[12], (L, Q_LORA, MLA_HEADS * (MLA_NOPE + MLA_ROPE)), Q_LORA ** -0.5)
    mla_kv_norm_g = 1.0 + nrm(ks[13], (L, KV_LORA), 0.01)
    mla_w_ukv = nrm(ks[14], (L, KV_LORA, MLA_HEADS * (MLA_NOPE + MLA_V)), KV_LORA ** -0.5)
    ssm_a_re = -0.5 + nrm(ks[15], (L, SSM_GROUPS, SSM_STATE), 0.01)
    ssm_a_im = (math.pi * jnp.arange(SSM_STATE, dtype=f32))[None, None, :] + nrm(ks[16], (L, SSM_GROUPS, SSM_STATE), 0.01)
    ssm_b_re = nrm(ks[17], (L, SSM_GROUPS, SSM_STATE, SSM_GROUP_CH), (2 * SSM_GROUP_CH) ** -0.5)
    ssm_b_im = nrm(ks[18], (L, SSM_GROUPS, SSM_STATE, SSM_GROUP_CH), (2 * SSM_GROUP_CH) ** -0.5)
    ssm_c_re = nrm(ks[19], (L, SSM_GROUPS, SSM_GROUP_CH, SSM_STATE), (2 * SSM_STATE) ** -0.5)
    ssm_c_im = nrm(ks[20], (L, SSM_GROUPS, SSM_GROUP_CH, SSM_STATE), (2 * SSM_STATE) ** -0.5)
    ssm_d = nrm(ks[21], (L, SSM_GROUPS, SSM_GROUP_CH), 1.0)
    ssm_log_dt = jax.random.uniform(ks[22], (L, SSM_GROUPS), f32, math.log(DT_MIN), math.log(DT_MAX))
    ssm_glu_w = nrm(ks[23], (L, SSM_WIDTH, SSM_WIDTH), SSM_WIDTH ** -0.5)
    ssm_glu_b = nrm(ks[24], (L, SSM_WIDTH), 0.01)
    mix_norm_g = 1.0 + nrm(ks[25], (L, D_MIX), 0.01)
    w_out = nrm(ks[26], (L, D_MIX, D), BETA * D_MIX ** -0.5)
    ffn2_w_gate = nrm(ks[27], (L, D, F), D ** -0.5)
    ffn2_w_up = nrm(ks[28], (L, D, F), D ** -0.5)
    ffn2_w_down = nrm(ks[29], (L, F, D), BETA * F ** -0.5)
    return {
        "x": x, "positions": positions, "ln_g": ln_g, "ln_b": ln_b,
        "ffn1_w_gate": ffn1_w_gate, "ffn1_w_up": ffn1_w_up, "ffn1_w_down": ffn1_w_down,
        "w_in": w_in,
        "gmlp_norm_g": gmlp_norm_g, "gmlp_ws": gmlp_ws, "gmlp_bs": gmlp_bs,
        "mla_q_norm_g": mla_q_norm_g, "mla_w_uq": mla_w_uq, "mla_kv_norm_g": mla_kv_norm_g, "mla_w_ukv": mla_w_ukv,
        "ssm_a_re": ssm_a_re, "ssm_a_im": ssm_a_im, "ssm_b_re": ssm_b_re, "ssm_b_im": ssm_b_im,
        "ssm_c_re": ssm_c_re, "ssm_c_im": ssm_c_im, "ssm_d": ssm_d, "ssm_log_dt": ssm_log_dt,
        "ssm_glu_w": ssm_glu_w, "ssm_glu_b": ssm_glu_b,
        "mix_norm_g": mix_norm_g, "w_out": w_out,
        "ffn2_w_gate": ffn2_w_gate, "ffn2_w_up": ffn2_w_up, "ffn2_w_down": ffn2_w_down,
    }


def reference(x, positions, ln_g, ln_b, ffn1_w_gate, ffn1_w_up, ffn1_w_down, w_in,
              gmlp_norm_g, gmlp_ws, gmlp_bs, mla_q_norm_g, mla_w_uq, mla_kv_norm_g, mla_w_ukv,
              ssm_a_re, ssm_a_im, ssm_b_re, ssm_b_im, ssm_c_re, ssm_c_im, ssm_d, ssm_log_dt,
              ssm_glu_w, ssm_glu_b, mix_norm_g, w_out, ffn2_w_gate, ffn2_w_up, ffn2_w_down):
    inv_freq = 1.0 / (ROPE_BASE ** (jnp.arange(0, MLA_ROPE, 2, dtype=jnp.float32) / MLA_ROPE))
    ang = positions.astype(jnp.float32)[..., None] * inv_freq
    cos, sin = jnp.cos(ang), jnp.sin(ang)
    o1 = 2 * GM_WIDTH
    o2 = o1 + Q_LORA
    o3 = o2 + KV_LORA
    o4 = o3 + MLA_ROPE
    for l in range(DEPTH):
        x = layer_norm(ALPHA * x + 0.5 * swiglu(x, ffn1_w_gate[l], ffn1_w_up[l], ffn1_w_down[l]), ln_g[l, 0], ln_b[l, 0])
        h = x @ w_in[l]
        y_a = gmlp_mixer(h[..., :GM_WIDTH], h[..., GM_WIDTH:o1], gmlp_norm_g[l], gmlp_ws[l], gmlp_bs[l])
        y_b = mla_mixer(h[..., o1:o2], h[..., o2:o3], h[..., o3:o4], cos, sin,
                        mla_q_norm_g[l], mla_w_uq[l], mla_kv_norm_g[l], mla_w_ukv[l])
        y_c = s5_mixer(h[..., o4:], ssm_a_re[l], ssm_a_im[l], ssm_b_re[l], ssm_b_im[l],
                       ssm_c_re[l], ssm_c_im[l], ssm_d[l], ssm_log_dt[l], ssm_glu_w[l], ssm_glu_b[l])
        y = jnp.concatenate([rms_only(y_a), rms_only(y_b), rms_only(y_c)], axis=-1) * mix_norm_g[l]
        x = layer_norm(ALPHA * x + y @ w_out[l], ln_g[l, 1], ln_b[l, 1])
        x = layer_norm(ALPHA * x + 0.5 * swiglu(x, ffn2_w_gate[l], ffn2_w_up[l], ffn2_w_down[l]), ln_g[l, 2], ln_b[l, 2])
    return x
```

```python
import math
from contextlib import ExitStack

import numpy as np
import concourse.bass as bass
import concourse.mybir as mybir
from concourse.bass_utils import run_bass_kernel_spmd

F32 = mybir.dt.float32
BF16 = mybir.dt.bfloat16
I32 = mybir.dt.int32
AF = mybir.ActivationFunctionType
ALU = mybir.AluOpType
AX = mybir.AxisListType

D = 1024
DFF = 2816
NFC = DFF // 128
DEPTH = 4
ALPHA = (2 * DEPTH) ** 0.25
LN_EPS = 1e-5
RMS_EPS = 1e-6
GM_W = 256
QL = 256
KVL = 128
ROPE = 32
SSM_W = 256
O1 = 2 * GM_W
O2 = O1 + QL
O3 = O2 + KVL
O4 = O3 + ROPE
IN_COLS = O4 + SSM_W
NH = 8
HD = 96
SM_SCALE = HD ** -0.5
GELU_C = 0.044715
GELU_S = 2.0 * math.sqrt(2.0 / math.pi)
USE_HW_SCAN = False
PI_IN = 3.1415925


class Buf:
    __slots__ = ("name", "w", "r", "dsem", "dcnt")

    def __init__(self, name):
        self.name = name
        self.w = None
        self.r = {}
        self.dsem = None
        self.dcnt = 0


class Prog:
    def __init__(self, nc, stack):
        self.nc = nc
        self.stack = stack
        self.eng = {"pe": nc.tensor, "act": nc.scalar, "dve": nc.vector, "pool": nc.gpsimd, "sp": nc.sync}
        self.sem = {}
        self.cnt = {}
        for k in self.eng:
            self.sem[k] = stack.enter_context(nc.semaphore("s_" + k))
            self.cnt[k] = 0
        self.waited = {}
        self.nsem = 5
        self.ninst = 0
        self.free_dsems = {}

    def _wait(self, e, toks):
        best = {}
        for t in toks:
            if t is None:
                continue
            sem, val, src = t
            if src == "pe" and e == "pe":
                continue
            k = id(sem)
            if k not in best or best[k][1] < val:
                best[k] = t
        for k, (sem, val, src) in best.items():
            wk = (e, k)
            if self.waited.get(wk, -1) >= val:
                continue
            self.eng[e].wait_ge(sem, val)
            self.waited[wk] = val

    def _deps(self, reads, writes):
        toks = []
        for b in reads:
            toks.append(b.w)
        for b in writes:
            toks.append(b.w)
            toks.extend(b.r.values())
        return toks

    def _mark(self, tok, reads, writes):
        k = id(tok[0])
        for b in reads:
            old = b.r.get(k)
            if old is None or old[1] < tok[1]:
                b.r[k] = tok
        for b in writes:
            b.w = tok
            b.r = {}

    def op(self, e, fn, reads=(), writes=()):
        self._wait(e, self._deps(reads, writes))
        ins = fn(self.eng[e])
        self.cnt[e] += 1
        ins.then_inc(self.sem[e], 1)
        tok = (self.sem[e], self.cnt[e], e)
        self._mark(tok, reads, writes)
        self.ninst += 1
        return tok

    def _get_dsem(self, b):
        if b.dsem is None:
            b.dsem = self.stack.enter_context(self.nc.semaphore("d%d" % self.nsem))
            self.nsem += 1
        return b.dsem

    def dma(self, q, pairs, reads=(), writes=(), owner=None):
        if owner is None:
            owner = writes[0] if writes else reads[0]
        self._wait(q, self._deps(reads, writes))
        sem = self._get_dsem(owner)
        for (o, i) in pairs:
            self.eng[q].dma_start(out=o, in_=i).then_inc(sem, 16)
            owner.dcnt += 16
            self.ninst += 1
        tok = (sem, owner.dcnt, "dma")
        self._mark(tok, reads, writes)
        return tok

    def finish(self, bufs):
        toks = []
        for b in bufs:
            toks.append(b.w)
            toks.extend(b.r.values())
        self._wait("sp", toks)
        self.eng["sp"].nop() if hasattr(self.eng["sp"], "nop") else None


class Tile:
    def __init__(self, P, name, shape, dt, psum=False):
        nc = P.nc
        if psum:
            self.t = P.stack.enter_context(nc.psum_tensor("sb_" + name, shape, dt))
        else:
            self.t = P.stack.enter_context(nc.sbuf_tensor("sb_" + name, shape, dt))
        self.b = Buf(name)
        self.shape = shape

    def __getitem__(self, idx):
        return self.t[idx]


class Ring:
    def __init__(self, P, name, n, shape, dt, psum=False):
        self.tiles = [Tile(P, "%s%d" % (name, i), shape, dt, psum) for i in range(n)]
        self.i = 0

    def next(self):
        t = self.tiles[self.i % len(self.tiles)]
        self.i += 1
        return t


class DramRegions:
    def __init__(self):
        self.d = {}

    def get(self, *key):
        b = self.d.get(key)
        if b is None:
            b = Buf("dram_" + "_".join(str(k) for k in key))
            self.d[key] = b
        return b


class Ctx:
    def __init__(self, nc, stack, S):
        self.nc = nc
        self.S = S
        self.P = Prog(nc, stack)
        self.stack = stack
        self.dr = DramRegions()
        self.uid = 0
        P = self.P
        self.ps = [Tile(P, "psb%d" % i, [128, 512], F32, psum=True) for i in range(8)]
        self.ident = Tile(P, "ident", [128, 128], F32)
        self.eps_ln = Tile(P, "eps_ln", [128, 1], F32)
        self.eps_rms = Tile(P, "eps_rms", [128, 1], F32)
        self.one = Tile(P, "one_c", [128, 1], F32)

    def load_consts(self, ident_ap):
        P = self.P
        P.dma("sp", [(self.ident[:], ident_ap)], writes=[self.ident.b])
        P.op("dve", lambda e: e.memset(self.eps_ln[:], LN_EPS), writes=[self.eps_ln.b])
        P.op("dve", lambda e: e.memset(self.eps_rms[:], RMS_EPS), writes=[self.eps_rms.b])
        P.op("dve", lambda e: e.memset(self.one[:], 1.0), writes=[self.one.b])


class PsRing:
    def __init__(self, tiles):
        self.tiles = tiles
        self.i = 0

    def next(self):
        t = self.tiles[self.i % len(self.tiles)]
        self.i += 1
        return t


class Phase:
    def __init__(self, C):
        self.C = C
        self.st = ExitStack()
        self.tiles = []

    def tile(self, name, shape, dt):
        P = self.C.P
        t = Tile.__new__(Tile)
        self.C.uid += 1
        t.t = self.st.enter_context(self.C.nc.sbuf_tensor("sb%d_%s" % (self.C.uid, name), shape, dt))
        t.b = Buf(name)
        t.shape = shape
        self.tiles.append(t)
        return t

    def ring(self, name, n, shape, dt):
        r = Ring.__new__(Ring)
        r.tiles = [self.tile("%s%d" % (name, i), shape, dt) for i in range(n)]
        r.i = 0
        return r

    def close(self):
        self.C.P.barrier()
        for t in self.tiles:
            self.C.P.release(t.b)
        self.st.close()


def _barrier(self):
    toks = [(self.sem[k], self.cnt[k], k) for k in self.eng if self.cnt[k] > 0]
    toks += [(s, c[0], "dma") for (s, c) in self.all_dsems]
    for e in self.eng:
        for (sem, val, src) in toks:
            if src == e:
                continue
            wk = (e, id(sem))
            if self.waited.get(wk, -1) >= val:
                continue
            self.eng[e].wait_ge(sem, val)
            self.waited[wk] = val


def _release(self, b):
    if b.dsem is not None:
        for kind, ds in b.dsem.items():
            self.free_dsems.setdefault(kind, []).append(ds)
        b.dsem = None


def _get_dsem2(self, b, kind):
    if b.dsem is None:
        b.dsem = {}
    if kind not in b.dsem:
        pool = self.free_dsems.setdefault(kind, [])
        if pool:
            b.dsem[kind] = pool.pop()
        else:
            sem = self.stack.enter_context(self.nc.semaphore("d%s%d" % (kind, self.nsem)))
            self.nsem += 1
            b.dsem[kind] = (sem, [0])
            self.all_dsems.append(b.dsem[kind])
    return b.dsem[kind]


def _dma2(self, q, pairs, reads=(), writes=(), owner=None):
    if owner is None:
        owner = writes[0] if writes else reads[0]
    self._wait(q, self._deps(reads, writes))
    sem, c = self._get_dsem(owner, "sw" if q == "pool" else "hw")
    for (o, i) in pairs:
        self.eng[q].dma_start(out=o, in_=i).then_inc(sem, 16)
        c[0] += 16
        self.ninst += 1
    tok = (sem, c[0], "dma")
    self._mark(tok, reads, writes)
    return tok


def _finish2(self, bufs):
    toks = []
    for b in bufs:
        toks.append(b.w)
        toks.extend(b.r.values())
    self._wait("sp", toks)


Prog.barrier = _barrier
Prog.release = _release
Prog._get_dsem = _get_dsem2
Prog.dma = _dma2
Prog.finish = _finish2
_old_init = Prog.__init__


def _init2(self, nc, stack):
    _old_init(self, nc, stack)
    self.all_dsems = []


Prog.__init__ = _init2


def emit_transpose_x(C, X, xT, xT_b, sub, psring, ncols=1024):
    P = C.P
    ndc = ncols // 128
    for q in range(0, ndc, 4):
        ps = psring.next()
        n = min(4, ndc - q)
        for j in range(n):
            dc = q + j
            P.op("pe", lambda e: e.transpose(out=ps[:, j * 128:(j + 1) * 128], in_=X[:, dc * 128:(dc + 1) * 128], identity=C.ident[:]),
                 reads=[X.b, C.ident.b], writes=[ps.b])
        src = ps[:, 0:n * 128].rearrange("p (j t) -> p j t", j=n)
        dst = xT[:, q:q + n, sub * 128:(sub + 1) * 128]
        eng = "act" if (q // 4) % 2 == 0 else "dve"
        if eng == "act":
            P.op("act", lambda e: e.copy(out=dst, in_=src), reads=[ps.b], writes=[xT_b])
        else:
            P.op("dve", lambda e: e.tensor_copy(out=dst, in_=src), reads=[ps.b], writes=[xT_b])


def emit_ln_epilogue(C, Z, mv, st6, rstd, Gbc, Bbc):
    P = C.P
    for h in range(2):
        P.op("dve", lambda e: e.bn_stats(out=st6[:, h, :], in_=Z[:, h * 512:(h + 1) * 512]), reads=[Z.b], writes=[st6.b])
    P.op("dve", lambda e: e.bn_aggr(out=mv[:], in_=st6[:]), reads=[st6.b], writes=[mv.b])
    emit_rsqrt(C, rstd[:], mv[:, 1:2], 1.0, C.eps_ln, [mv.b], [rstd.b])
    P.op("dve", lambda e: e.tensor_scalar(out=Z[:], in0=Z[:], scalar1=mv[:, 0:1], scalar2=rstd[:, 0:1], op0=ALU.subtract, op1=ALU.mult),
         reads=[Z.b, mv.b, rstd.b], writes=[Z.b])
    P.op("dve", lambda e: e.tensor_tensor(out=Z[:], in0=Z[:], in1=Gbc[:], op=ALU.mult), reads=[Z.b, Gbc.b], writes=[Z.b])
    P.op("dve", lambda e: e.tensor_tensor(out=Z[:], in0=Z[:], in1=Bbc[:], op=ALU.add), reads=[Z.b, Bbc.b], writes=[Z.b])


def phase_ffn(C, x_in, x_out, xin_key, xout_key, wg, wu, wd, lng, lnb):
    P, S = C.P, C.S
    G = min(1024, S)
    NG = S // G
    NSUB = G // 128
    NH2 = G // 512
    ph = Phase(C)
    Wd = ph.tile("Wd", [128, NFC, D], BF16)
    WG = ph.ring("WG", 2, [128, 8, 256], BF16)
    WU = ph.ring("WU", 2, [128, 8, 256], BF16)
    WST = ph.ring("WST", 2, [128, 8, 256], F32)
    XR = ph.ring("XR", 2, [128, D], F32)
    X2 = ph.ring("X2", 2, [128, D], F32)
    ZR = ph.ring("ZR", 2, [128, D], F32)
    xT = ph.tile("xT", [128, 8, G], BF16)
    xT_b = [Buf("xT%d" % i) for i in range(NSUB)]
    gT = ph.tile("gT", [128, NFC, G], BF16)
    gT_b = [Buf("gT%d" % i) for i in range(NFC)]
    SG = ph.ring("SG", 2, [128, 512], F32)
    Gbc = ph.tile("Gbc", [128, D], F32)
    Bbc = ph.tile("Bbc", [128, D], F32)
    st6 = ph.ring("st6", 2, [128, 2, 6], F32)
    mv = ph.ring("mv", 2, [128, 2], F32)
    rstd = ph.ring("rstd", 2, [128, 1], F32)
    ps_t = PsRing(C.ps[0:2])
    ps_gu = PsRing(C.ps[2:6])
    ps_d = PsRing(C.ps[6:8])

    wd_v = wd.rearrange("(c p) d -> p c d", p=128)
    Wd_b = [Buf("Wd%d" % i) for i in range(NFC)]
    for q in range(0, NFC, 2):
        load_cast(P, Wd[:, q:q + 2, :], Wd_b[q], wd_v[:, q:q + 2, :], WST, "pool" if (q // 2) % 2 == 0 else "dve",
                  view=lambda t: t[:].rearrange("p a f -> p (a f)").rearrange("p (c d) -> p c d", c=2))
        Wd_b[q + 1] = Wd_b[q]
    P.dma("sp", [(Gbc[:], lng.rearrange("(o d) -> o d", o=1).broadcast_to([128, D]))], writes=[Gbc.b])
    P.dma("sp", [(Bbc[:], lnb.rearrange("(o d) -> o d", o=1).broadcast_to([128, D]))], writes=[Bbc.b])
    wg_v = wg.rearrange("(c p) f -> p c f", p=128)
    wu_v = wu.rearrange("(c p) f -> p c f", p=128)

    for g in range(NG):
        t0 = g * G
        for s in range(NSUB):
            X = XR.next()
            r0 = t0 + s * 128
            P.dma("sp", [(X[:], x_in[r0:r0 + 128, :])], reads=[C.dr.get(xin_key, r0)], writes=[X.b])
            emit_transpose_x(C, X, xT, xT_b[s], s, ps_t)
        for fc2 in range(NFC // 2):
            wgt = WG.next()
            wut = WU.next()
            load_cast(P, wgt[:], wgt.b, wg_v[:, :, fc2 * 256:(fc2 + 1) * 256], WST, "pool")
            load_cast(P, wut[:], wut.b, wu_v[:, :, fc2 * 256:(fc2 + 1) * 256], WST, "pool")
            for sf in range(2):
                fc = fc2 * 2 + sf
                for h in range(NH2):
                    pg = ps_gu.next()
                    pu = ps_gu.next()
                    xb = xT_b[h * 4:(h + 1) * 4]
                    for (pt, wt) in ((pg, wgt), (pu, wut)):
                        for dc in range(8):
                            P.op("pe", lambda e: e.matmul(pt[:, :], lhsT=wt[:, dc, sf * 128:(sf + 1) * 128], rhs=xT[:, dc, h * 512:(h + 1) * 512],
                                                          start=(dc == 0), stop=(dc == 7)),
                                 reads=[wt.b] + xb, writes=[pt.b])
                    sg = SG.next()
                    emit_sigmoid(C, sg[:], pg[:, :], [pg.b], [sg.b])
                    TT(P, "dve", sg[:], sg[:], pg[:, :], ALU.mult, reads=[sg.b, pg.b], writes=[sg.b])
                    TT(P, "dve", gT[:, fc, h * 512:(h + 1) * 512], sg[:], pu[:, :], ALU.mult, reads=[sg.b, pu.b], writes=[gT_b[fc]])
        for s in range(NSUB):
            r0 = t0 + s * 128
            X = X2.next()
            P.dma("sp", [(X[:], x_in[r0:r0 + 128, :])], reads=[C.dr.get(xin_key, r0)], writes=[X.b])
            Z = ZR.next()
            for dh in range(2):
                pd = ps_d.next()
                for fc in range(NFC):
                    P.op("pe", lambda e: e.matmul(pd[:, :], lhsT=gT[:, fc, s * 128:(s + 1) * 128], rhs=Wd[:, fc, dh * 512:(dh + 1) * 512],
                                                  start=(fc == 0), stop=(fc == NFC - 1)),
                         reads=[gT_b[fc], Wd_b[fc]], writes=[pd.b])
                P.op("act", lambda e: e.activation(out=Z[:, dh * 512:(dh + 1) * 512], in_=pd[:, :], func=AF.Copy, scale=0.5),
                     reads=[pd.b], writes=[Z.b])
            P.op("dve", lambda e: e.scalar_tensor_tensor(out=Z[:], in0=X[:], scalar=ALPHA, in1=Z[:], op0=ALU.mult, op1=ALU.add),
                 reads=[X.b, Z.b], writes=[Z.b])
            emit_ln_epilogue(C, Z, mv.next(), st6.next(), rstd.next(), Gbc, Bbc)
            P.dma("sp", [(x_out[r0:r0 + 128, :], Z[:])], reads=[Z.b], writes=[C.dr.get(xout_key, r0)], owner=Z.b)
    ph.close()


def TT(P, eng, out, in0, in1, op, reads, writes):
    return P.op(eng, lambda e: e.tensor_tensor(out=out, in0=in0, in1=in1, op=op), reads=reads, writes=writes)


def TS(P, eng, out, in0, s1, s2, op0, op1, reads, writes):
    if op1 is None:
        return P.op(eng, lambda e: e.tensor_scalar(out=out, in0=in0, scalar1=s1, scalar2=None, op0=op0), reads=reads, writes=writes)
    return P.op(eng, lambda e: e.tensor_scalar(out=out, in0=in0, scalar1=s1, scalar2=s2, op0=op0, op1=op1), reads=reads, writes=writes)


def STT(P, eng, out, in0, scalar, in1, op0, op1, reads, writes):
    return P.op(eng, lambda e: e.scalar_tensor_tensor(out=out, in0=in0, scalar=scalar, in1=in1, op0=op0, op1=op1), reads=reads, writes=writes)


def ACTF(P, out, in_, func, reads, writes, bias=None, scale=None, accum_out=None):
    kw = {}
    if bias is not None:
        kw["bias"] = bias
    if scale is not None:
        kw["scale"] = scale
    if accum_out is not None:
        kw["accum_out"] = accum_out
    return P.op("act", lambda e: e.activation(out=out, in_=in_, func=func, **kw), reads=reads, writes=writes)


def CP(P, eng, out, in_, reads, writes):
    if eng == "act":
        return P.op("act", lambda e: e.copy(out=out, in_=in_), reads=reads, writes=writes)
    return P.op(eng, lambda e: e.tensor_copy(out=out, in_=in_), reads=reads, writes=writes)


def MM(P, out, lhsT, rhs, start, stop, reads, writes):
    return P.op("pe", lambda e: e.matmul(out, lhsT=lhsT, rhs=rhs, start=start, stop=stop), reads=reads, writes=writes)


def load_cast(P, dst, dst_b, src, stage_ring, eng, view=None):
    s_ = stage_ring.next()
    sv = s_[:] if view is None else view(s_)
    P.dma("sp", [(sv, src)], writes=[s_.b])
    CP(P, eng, dst, sv, reads=[s_.b], writes=[dst_b])


def emit_rsqrt(C, out, in_, scale, eps_tile, reads, writes):
    P = C.P
    ACTF(P, out, in_, AF.Ln, reads=reads + [eps_tile.b], writes=writes, bias=eps_tile[:], scale=scale)
    ACTF(P, out, out, AF.Exp, reads=writes, writes=writes, scale=-0.5)


def emit_sigmoid(C, out, in_, reads, writes, nbias=None, nbias_b=None):
    P = C.P
    if nbias is None:
        ACTF(P, out, in_, AF.Exp, reads=reads, writes=writes, scale=-1.0)
    else:
        ACTF(P, out, in_, AF.Exp, reads=reads + [nbias_b], writes=writes, bias=nbias, scale=-1.0)
    ACTF(P, out, out, AF.Ln, reads=writes + [C.one.b], writes=writes, bias=C.one[:], scale=1.0)
    ACTF(P, out, out, AF.Exp, reads=writes, writes=writes, scale=-1.0)


def emit_gelu(C, dst, src, xs, t1, t2, rd, wr_dst, xsb, t1b, t2b):
    P = C.P
    if xs is not None:
        P.op("act", lambda e: e.copy(out=xs, in_=src), reads=rd, writes=[xsb])
        x, xr = xs, [xsb]
    else:
        x, xr = src, rd
    TT(P, "dve", t1, x, x, ALU.mult, reads=xr, writes=[t1b])
    TS(P, "dve", t1, t1, GELU_C * GELU_S, GELU_S, ALU.mult, ALU.add, reads=[t1b], writes=[t1b])
    TT(P, "dve", t2, t1, x, ALU.mult, reads=[t1b] + xr, writes=[t2b])
    TS(P, "dve", t2, t2, -43.0, None, ALU.max, None, reads=[t2b], writes=[t2b])
    emit_sigmoid(C, t2, t2, [t2b], [t2b])
    TT(P, "dve", dst, t2, x, ALU.mult, reads=[t2b] + xr, writes=wr_dst)


_SIN_C = [-1.0 / 39916800, 1.0 / 362880, -1.0 / 5040, 1.0 / 120, -1.0 / 6, 1.0]
_COS_C = [1.0 / 479001600, -1.0 / 3628800, 1.0 / 40320, -1.0 / 720, 1.0 / 24, -0.5, 1.0]
_MAGIC = 12582912.0


def emit_sincos(P, sin_out, cos_out, ang, kf, r, h, p, S, Cp, bufs):
    TWO_PI = 2.0 * math.pi
    C1 = 6.28125
    C2 = TWO_PI - C1
    B = bufs
    TS(P, "dve", kf, ang, 1.0 / TWO_PI, None, ALU.mult, None, reads=[B["ang"]], writes=[B["kf"]])
    TS(P, "dve", kf, kf, _MAGIC, None, ALU.add, None, reads=[B["kf"]], writes=[B["kf"]])
    TS(P, "dve", kf, kf, -_MAGIC, None, ALU.add, None, reads=[B["kf"]], writes=[B["kf"]])
    STT(P, "dve", r, kf, -C1, ang, ALU.mult, ALU.add, reads=[B["kf"], B["ang"]], writes=[B["r"]])
    STT(P, "dve", r, kf, -C2, r, ALU.mult, ALU.add, reads=[B["kf"], B["r"]], writes=[B["r"]])
    TS(P, "dve", h, r, 0.5, None, ALU.mult, None, reads=[B["r"]], writes=[B["h"]])
    TT(P, "dve", p, h, h, ALU.mult, reads=[B["h"]], writes=[B["p"]])
    TS(P, "dve", S, p, _SIN_C[0], _SIN_C[1], ALU.mult, ALU.add, reads=[B["p"]], writes=[B["S"]])
    for c in _SIN_C[2:]:
        TT(P, "dve", S, S, p, ALU.mult, reads=[B["S"], B["p"]], writes=[B["S"]])
        TS(P, "dve", S, S, c, None, ALU.add, None, reads=[B["S"]], writes=[B["S"]])
    TT(P, "dve", S, S, h, ALU.mult, reads=[B["S"], B["h"]], writes=[B["S"]])
    TS(P, "dve", Cp, p, _COS_C[0], _COS_C[1], ALU.mult, ALU.add, reads=[B["p"]], writes=[B["Cp"]])
    for c in _COS_C[2:]:
        TT(P, "dve", Cp, Cp, p, ALU.mult, reads=[B["Cp"], B["p"]], writes=[B["Cp"]])
        TS(P, "dve", Cp, Cp, c, None, ALU.add, None, reads=[B["Cp"]], writes=[B["Cp"]])
    STT(P, "dve", sin_out, S, 2.0, Cp, ALU.mult, ALU.mult, reads=[B["S"], B["Cp"]], writes=[B["sin"]])
    TT(P, "dve", cos_out, S, S, ALU.mult, reads=[B["S"]], writes=[B["cos"]])
    TS(P, "dve", cos_out, cos_out, -2.0, 1.0, ALU.mult, ALU.add, reads=[B["cos"]], writes=[B["cos"]])


def phase_rope_tables(C, positions, invf, ROT):
    P, S = C.P, C.S
    ph = Phase(C)
    CH = min(S, 2048)
    pos_i = ph.tile("pos_i", [32, CH], I32)
    names = ("ang", "kf", "r", "h", "p", "S", "Cp", "sin", "cos")
    T_ = {n: ph.tile("rp_" + n, [32, CH], F32) for n in names}
    bufs = {n: T_[n].b for n in names}
    iv = ph.tile("iv", [32, 1], F32)
    P.dma("sp", [(iv[:], invf)], writes=[iv.b])
    for c0 in range(0, S, CH):
        P.dma("sp", [(pos_i[:], positions[c0:c0 + CH].rearrange("(o s) -> o s", o=1).broadcast_to([32, CH]))], writes=[pos_i.b])
        CP(P, "dve", T_["ang"][:], pos_i[:], reads=[pos_i.b], writes=[bufs["ang"]])
        TS(P, "dve", T_["ang"][:], T_["ang"][:], iv[:, 0:1], None, ALU.mult, None, reads=[bufs["ang"], iv.b], writes=[bufs["ang"]])
        emit_sincos(P, T_["sin"][:], T_["cos"][:], T_["ang"][:], T_["kf"][:], T_["r"][:], T_["h"][:], T_["p"][:], T_["S"][:], T_["Cp"][:], bufs)
        P.dma("sp", [(ROT[1, :, c0:c0 + CH], T_["sin"][:])], reads=[bufs["sin"]], owner=T_["sin"].b)
        P.dma("sp", [(ROT[0, :, c0:c0 + CH], T_["cos"][:])], reads=[bufs["cos"]], owner=T_["cos"].b)
    ph.close()


def phase_proj(C, x_in, W, l, ROT, YT, QT, KT, VA, UT, tril):
    P, S = C.P, C.S
    G = 512
    NG = S // G
    ph = Phase(C)
    Win = ph.tile("Win", [128, 8, IN_COLS], BF16)
    win_v = W["w_in"][l].rearrange("(c p) f -> p c f", p=128)
    wst = ph.ring("wst", 2, [128, IN_COLS], F32)
    for dc in range(8):
        load_cast(P, Win[:, dc, :], Win.b, win_v[:, dc, :], wst, "pool" if dc % 2 == 0 else "dve")
    Wkr = ph.tile("Wkr", [128, 8, 96], BF16)
    Wkrr = ph.tile("Wkrr", [128, 8, 96], BF16)
    P.op("dve", lambda e: e.memset(Wkr[:], 0.0), writes=[Wkr.b])
    P.op("dve", lambda e: e.memset(Wkrr[:], 0.0), writes=[Wkrr.b])
    CP(P, "dve", Wkr[:, :, 64:96], Win[:, :, O3:O4], reads=[Win.b], writes=[Wkr.b])
    TS(P, "dve", Wkrr[:, :, 64:80], Win[:, :, O3 + 16:O4], -1.0, None, ALU.mult, None, reads=[Win.b], writes=[Wkrr.b])
    CP(P, "dve", Wkrr[:, :, 80:96], Win[:, :, O3:O3 + 16], reads=[Win.b], writes=[Wkrr.b])
    stg = ph.tile("stg", [128, 2, 768], F32)
    gq = ph.tile("gq", [128, 2], F32)
    P.dma("sp", [(stg[:], W["mla_w_uq"][l].rearrange("(k p) f -> p k f", p=128))], writes=[stg.b])
    with C.nc.allow_non_contiguous_dma(reason="tiny gain vectors"):
        P.dma("sp", [(gq[:], W["mla_q_norm_g"][l].rearrange("(k p) -> p k", p=128))], writes=[gq.b])
    Wq = ph.tile("Wq", [128, 2, 768], BF16)
    Wqr = ph.tile("Wqr", [128, 2, 768], BF16)
    P.op("dve", lambda e: e.memset(Wqr[:], 0.0), writes=[Wqr.b])
    for k in range(2):
        TS(P, "dve", stg[:, k, :], stg[:, k, :], SM_SCALE, None, ALU.mult, None, reads=[stg.b], writes=[stg.b])
        TS(P, "dve", Wq[:, k, :], stg[:, k, :], gq[:, k:k + 1], None, ALU.mult, None, reads=[stg.b, gq.b], writes=[Wq.b])
    Wq4 = Wq[:].rearrange("p k (h d) -> p k h d", h=NH)
    Wqr4 = Wqr[:].rearrange("p k (h d) -> p k h d", h=NH)
    for k in range(2):
        TS(P, "dve", Wqr4[:, k, :, 64:80], Wq4[:, k, :, 80:96], -1.0, None, ALU.mult, None, reads=[Wq.b], writes=[Wqr.b])
        CP(P, "dve", Wqr4[:, k, :, 80:96], Wq4[:, k, :, 64:80], reads=[Wq.b], writes=[Wqr.b])
    stg2 = ph.tile("stg2", [128, 1024], F32)
    gkv = ph.tile("gkv", [128, 1], F32)
    P.dma("sp", [(stg2[:], W["mla_w_ukv"][l])], writes=[stg2.b])
    with C.nc.allow_non_contiguous_dma(reason="tiny gain vectors"):
        P.dma("sp", [(gkv[:], W["mla_kv_norm_g"][l].rearrange("(p o) -> p o", o=1))], writes=[gkv.b])
    Wkn = ph.tile("Wkn", [128, 512], BF16)
    Wv = ph.tile("Wv", [128, 512], BF16)
    stg2v = stg2[:].rearrange("p (h d) -> p h d", h=NH)
    TS(P, "dve", Wkn[:].rearrange("p (h d) -> p h d", h=NH), stg2v[:, :, 0:64], gkv[:, 0:1], None, ALU.mult, None, reads=[stg2.b, gkv.b], writes=[Wkn.b])
    TS(P, "dve", Wv[:].rearrange("p (h d) -> p h d", h=NH), stg2v[:, :, 64:128], gkv[:, 0:1], None, ALU.mult, None, reads=[stg2.b, gkv.b], writes=[Wv.b])
    wsf = ph.tile("wsf", [128, 4, 128], F32)
    trl = ph.tile("trl", [128, 128], F32)
    WsT = ph.tile("WsT", [128, 4, 128], BF16)
    bs = ph.tile("bs", [128, 4], F32)
    Gv = ph.tile("Gv", [128, 256], F32)
    P.dma("sp", [(wsf[:], W["gmlp_ws"][l].rearrange("h t c -> t h c"))], writes=[wsf.b])
    P.dma("sp", [(trl[:], tril)], writes=[trl.b])
    with C.nc.allow_non_contiguous_dma(reason="tiny bias vectors"):
        P.dma("sp", [(bs[:], W["gmlp_bs"][l].rearrange("h t -> t h"))], writes=[bs.b])
    P.dma("sp", [(Gv[:], W["gmlp_norm_g"][l].rearrange("(o d) -> o d", o=1).broadcast_to([128, 256]))], writes=[Gv.b])
    for h in range(4):
        TT(P, "dve", wsf[:, h, :], wsf[:, h, :], trl[:], ALU.mult, reads=[wsf.b, trl.b], writes=[wsf.b])
    pst = C.ps[0]
    for h in range(4):
        P.op("pe", lambda e: e.transpose(out=pst[:, h * 128:(h + 1) * 128], in_=wsf[:, h, :], identity=C.ident[:]),
             reads=[wsf.b, C.ident.b], writes=[pst.b])
    CP(P, "dve", WsT[:].rearrange("p h t -> p (h t)"), pst[:, :], reads=[pst.b], writes=[WsT.b])
    on256 = ph.tile("on256", [128, 128], BF16)
    on128 = ph.tile("on128", [128, 128], BF16)
    P.op("dve", lambda e: e.memset(on256[:], 1.0 / 256), writes=[on256.b])
    P.op("dve", lambda e: e.memset(on128[:], 1.0 / 128), writes=[on128.b])

    XR = ph.ring("XR", 3, [128, D], F32)
    xT = ph.tile("xT", [128, 8, G], BF16)
    xT_b = [Buf("xT%d" % i) for i in range(4)]
    ROTc = ph.ring("ROTc", 2, [128, G], F32)
    ROTs = ph.ring("ROTs", 2, [128, G], F32)
    t1 = ph.ring("t1", 2, [128, 512], F32)
    xsr = ph.ring("xsr", 2, [128, 512], F32)
    t2 = ph.ring("t2", 2, [128, 512], F32)
    guv = ph.ring("guv", 2, [128, 512], F32)
    sq = ph.ring("sq", 2, [128, 256], F32)
    s1 = ph.ring("s1", 2, [128, 4], F32)
    s2 = ph.ring("s2", 2, [128, 4], F32)
    mu = ph.ring("mu", 2, [128, 4], F32)
    var = ph.ring("var", 2, [128, 4], F32)
    vc = ph.ring("vc", 2, [128, 256], F32)
    vn = ph.ring("vn", 2, [128, 256], BF16)
    ya = ph.ring("ya", 2, [128, 256], F32)
    ssq = ph.ring("ssq", 2, [128, 1], F32)
    bst = ph.ring("bst", 6, [128, 6], F32)
    bmv = ph.ring("bmv", 6, [128, 2], F32)
    yaT = ph.ring("yaT", 2, [128, 2, G], BF16)
    usb = ph.ring("usb", 2, [128, 2, G], F32)
    csb = ph.ring("csb", 2, [128, 3, G], F32)
    csq = ph.ring("csq", 2, [128, 3, G], BF16)
    rbc = ph.ring("rbc", 2, [128, 2, G], F32)
    cn = ph.ring("cn", 2, [128, 3, G], BF16)
    knb = ph.ring("knb", 2, [128, G], BF16)
    vsb = ph.ring("vsb", 2, [128, 512], BF16)
    krb = ph.ring("krb", 2, [128, G], BF16)
    rtmp = ph.ring("rtmp", 3, [128, G], F32)
    qtb = ph.ring("qtb", 3, [128, G], BF16)
    ps_t = PsRing(C.ps[0:2])
    ps_g = PsRing(C.ps[2:8])

    for g in range(NG):
        c0 = g * G
        cs = slice(c0, c0 + G)
        rc = ROTc.next()
        rs = ROTs.next()
        P.dma("sp", [(rc[64:96, :], ROT[0, :, cs])], writes=[rc.b])
        P.dma("sp", [(rs[64:96, :], ROT[1, :, cs])], writes=[rs.b])
        for s in range(4):
            X = XR.next()
            r0 = c0 + s * 128
            P.dma("sp", [(X[:], x_in[r0:r0 + 128, :])], writes=[X.b])
            emit_transpose_x(C, X, xT, xT_b[s], s, ps_t)

        c_t = csb.next()
        u_t = usb.next()
        q_t = csq.next()
        for i, col in enumerate((O1, O1 + 128, O2, O4, O4 + 128)):
            pp = ps_g.next()
            for dc in range(8):
                MM(P, pp[:, :], Win[:, dc, col:col + 128], xT[:, dc, :], dc == 0, dc == 7, reads=[Win.b] + xT_b, writes=[pp.b])
            if i < 3:
                CP(P, "act", c_t[:, i, :], pp[:, :], reads=[pp.b], writes=[c_t.b])
                TT(P, "dve", q_t[:, i, :], pp[:, :], c_t[:, i, :], ALU.mult, reads=[pp.b, c_t.b], writes=[q_t.b])
            else:
                CP(P, "act", u_t[:, i - 3, :], pp[:, :], reads=[pp.b], writes=[u_t.b])
        P.dma("sp", [(UT[:, cs].rearrange("(k p) t -> p k t", p=128), u_t[:])], reads=[u_t.b])
        r_t = rbc.next()
        pq = ps_g.next()
        MM(P, pq[:, :], on256[:], q_t[:, 0, :], True, False, reads=[on256.b, q_t.b], writes=[pq.b])
        MM(P, pq[:, :], on256[:], q_t[:, 1, :], False, True, reads=[on256.b, q_t.b], writes=[pq.b])
        emit_rsqrt(C, r_t[:, 0, :], pq[:, :], 1.0, C.eps_rms, [pq.b], [r_t.b])
        pk = ps_g.next()
        MM(P, pk[:, :], on128[:], q_t[:, 2, :], True, True, reads=[on128.b, q_t.b], writes=[pk.b])
        emit_rsqrt(C, r_t[:, 1, :], pk[:, :], 1.0, C.eps_rms, [pk.b], [r_t.b])
        n_t = cn.next()
        for i in range(3):
            TT(P, "dve", n_t[:, i, :], c_t[:, i, :], r_t[:, 0 if i < 2 else 1, :], ALU.mult, reads=[c_t.b, r_t.b], writes=[n_t.b])
        for hp in range(4):
            pp = ps_g.next()
            MM(P, pp[:, :], Wkn[:, hp * 128:(hp + 1) * 128], n_t[:, 2, :], True, True, reads=[Wkn.b, n_t.b], writes=[pp.b])
            kb_ = knb.next()
            CP(P, "act", kb_[:], pp[:, :], reads=[pp.b], writes=[kb_.b])
            P.dma("sp", [(KT[2 * hp, 0:64, cs], kb_[0:64, :]), (KT[2 * hp + 1, 0:64, cs], kb_[64:128, :])], reads=[kb_.b])
        for s in range(4):
            pp = ps_g.next()
            MM(P, pp[:, :], n_t[:, 2, s * 128:(s + 1) * 128], Wv[:], True, True, reads=[Wv.b, n_t.b], writes=[pp.b])
            v_ = vsb.next()
            CP(P, "act", v_[:], pp[:, :], reads=[pp.b], writes=[v_.b])
            P.dma("sp", [(VA[c0 + s * 128:c0 + (s + 1) * 128, :], v_[:])], reads=[v_.b])
        pa = ps_g.next()
        pb = ps_g.next()
        for dc in range(8):
            MM(P, pa[0:96, :], Wkr[:, dc, :], xT[:, dc, :], dc == 0, dc == 7, reads=[Wkr.b] + xT_b, writes=[pa.b])
        for dc in range(8):
            MM(P, pb[0:96, :], Wkrr[:, dc, :], xT[:, dc, :], dc == 0, dc == 7, reads=[Wkrr.b] + xT_b, writes=[pb.b])
        ra = rtmp.next()
        rb = rtmp.next()
        kr_ = krb.next()
        TT(P, "dve", ra[64:96, :], pa[64:96, :], rc[64:96, :], ALU.mult, reads=[pa.b, rc.b], writes=[ra.b])
        TT(P, "dve", rb[64:96, :], pb[64:96, :], rs[64:96, :], ALU.mult, reads=[pb.b, rs.b], writes=[rb.b])
        TT(P, "dve", kr_[64:96, :], ra[64:96, :], rb[64:96, :], ALU.add, reads=[ra.b, rb.b], writes=[kr_.b])
        P.dma("sp", [(KT[h, 64:96, cs], kr_[64:96, :]) for h in range(NH)], reads=[kr_.b])
        for h in range(NH):
            pa = ps_g.next()
            pb = ps_g.next()
            for k in range(2):
                MM(P, pa[0:96, :], Wq[:, k, h * 96:(h + 1) * 96], n_t[:, k, :], k == 0, k == 1, reads=[Wq.b, n_t.b], writes=[pa.b])
            for k in range(2):
                MM(P, pb[0:96, :], Wqr[:, k, h * 96:(h + 1) * 96], n_t[:, k, :], k == 0, k == 1, reads=[Wqr.b, n_t.b], writes=[pb.b])
            qt_ = qtb.next()
            ra = rtmp.next()
            rb = rtmp.next()
            CP(P, "act", qt_[0:64, :], pa[0:64, :], reads=[pa.b], writes=[qt_.b])
            TT(P, "dve", ra[64:96, :], pa[64:96, :], rc[64:96, :], ALU.mult, reads=[pa.b, rc.b, qt_.b], writes=[ra.b])
            TT(P, "dve", rb[64:96, :], pb[64:96, :], rs[64:96, :], ALU.mult, reads=[pb.b, rs.b], writes=[rb.b])
            TT(P, "dve", qt_[64:96, :], ra[64:96, :], rb[64:96, :], ALU.add, reads=[ra.b, rb.b], writes=[qt_.b])
            P.dma("sp", [(QT[h, :, cs], qt_[0:96, :])], reads=[qt_.b])

        yT = yaT.next()
        for s in range(4):
            pp = ps_g.next()
            for dc in range(8):
                MM(P, pp[:, :], xT[:, dc, s * 128:(s + 1) * 128], Win[:, dc, 0:512], dc == 0, dc == 7, reads=[Win.b, xT_b[s]], writes=[pp.b])
            a1 = t1.next()
            a2 = t2.next()
            gg = guv.next()
            xs_ = xsr.next()
            emit_gelu(C, gg[:], pp[:, :], xs_[:], a1[:], a2[:], [pp.b], [gg.b], xs_.b, a1.b, a2.b)
            vc_, vn_ = vc.next(), vn.next()
            for h in range(4):
                st_, mv_ = bst.next(), bmv.next()
                P.op("dve", lambda e: e.bn_stats(out=st_[:], in_=gg[:, 256 + h * 64:256 + (h + 1) * 64]), reads=[gg.b], writes=[st_.b])
                P.op("dve", lambda e: e.bn_aggr(out=mv_[:], in_=st_[:]), reads=[st_.b], writes=[mv_.b])
                emit_rsqrt(C, mv_[:, 1:2], mv_[:, 1:2], 1.0, C.eps_ln, [mv_.b], [mv_.b])
                TS(P, "dve", vc_[:, h * 64:(h + 1) * 64], gg[:, 256 + h * 64:256 + (h + 1) * 64], mv_[:, 0:1], mv_[:, 1:2], ALU.subtract, ALU.mult,
                   reads=[gg.b, mv_.b], writes=[vc_.b])
            TT(P, "dve", vn_[:], vc_[:], Gv[:], ALU.mult, reads=[vc_.b, Gv.b], writes=[vn_.b])
            pz = ps_g.next()
            for h in range(4):
                MM(P, pz[:, h * 64:(h + 1) * 64], WsT[:, h, :], vn_[:, h * 64:(h + 1) * 64], True, True, reads=[WsT.b, vn_.b], writes=[pz.b])
            ya_ = ya.next()
            for h in range(4):
                STT(P, "dve", ya_[:, h * 64:(h + 1) * 64], pz[:, h * 64:(h + 1) * 64], bs[:, h:h + 1], gg[:, h * 64:(h + 1) * 64], ALU.add, ALU.mult,
                    reads=[pz.b, bs.b, gg.b], writes=[ya_.b])
            st_, mv_, ss_ = bst.next(), bmv.next(), ssq.next()
            P.op("dve", lambda e: e.bn_stats(out=st_[:], in_=ya_[:]), reads=[ya_.b], writes=[st_.b])
            P.op("dve", lambda e: e.bn_aggr(out=mv_[:], in_=st_[:]), reads=[st_.b], writes=[mv_.b])
            STT(P, "dve", ss_[:], mv_[:, 0:1], mv_[:, 0:1], mv_[:, 1:2], ALU.mult, ALU.add, reads=[mv_.b], writes=[ss_.b])
            emit_rsqrt(C, ss_[:], ss_[:], 1.0, C.eps_rms, [ss_.b], [ss_.b])
            TS(P, "dve", ya_[:], ya_[:], ss_[:, 0:1], None, ALU.mult, None, reads=[ya_.b, ss_.b], writes=[ya_.b])
            py = ps_t.next()
            for k in range(2):
                P.op("pe", lambda e: e.transpose(out=py[:, k * 128:(k + 1) * 128], in_=ya_[:, k * 128:(k + 1) * 128], identity=C.ident[:]),
                     reads=[ya_.b, C.ident.b], writes=[py.b])
            CP(P, "act", yT[:, :, s * 128:(s + 1) * 128], py[:, 0:256].rearrange("p (k t) -> p k t", k=2), reads=[py.b], writes=[yT.b])
        P.dma("sp", [(YT[0:256, cs].rearrange("(k p) t -> p k t", p=128), yT[:])], reads=[yT.b])
    ph.close()


def phase_attn(C, QT, KT, VA, YT, triu, shiftm):
    P, S = C.P, C.S
    NQG = S // 512
    NKB = S // 128
    ph = Phase(C)
    Kh = ph.ring("Kh", 2, [128, S], BF16)
    Vh = ph.ring("Vh", 2, [128, NKB, 128], BF16)
    for v in Vh.tiles:
        P.op("dve", lambda e: e.memset(v[:, :, 64:128], 1.0), writes=[v.b])
    tri = ph.tile("tri", [128, 128], BF16)
    shf32 = ph.tile("shf32", [128, 128], F32)
    shf = ph.tile("shf", [128, 128], BF16)
    trf = ph.tile("trf", [128, 128], F32)
    P.dma("sp", [(trf[:], triu)], writes=[trf.b])
    CP(P, "dve", tri[:], trf[:], reads=[trf.b], writes=[tri.b])
    P.dma("sp", [(shf32[:], shiftm)], writes=[shf32.b])
    CP(P, "dve", shf[:], shf32[:], reads=[shf32.b], writes=[shf.b])
    qT = ph.ring("qT", 3, [128, 512], BF16)
    PT = ph.ring("PT", 4, [128, 512], BF16)
    ER = ph.ring("ER", 2, [128, 512], F32)
    EH = ph.ring("EH", 2, [128, 512], BF16)
    EL = ph.ring("EL", 2, [128, 512], BF16)
    for t_ in EH.tiles + EL.tiles:
        P.op("dve", lambda e: e.memset(t_[:], 0.0), writes=[t_.b])
    YB = ph.ring("YB", 2, [128, 512], BF16)
    ps_s = PsRing(C.ps[0:3])
    ps_o = PsRing(C.ps[3:6])
    ps_r = PsRing(C.ps[6:8])

    blocks = [(h, j, kb) for h in range(NH) for j in range(NQG) for kb in range(4 * (j + 1))]
    cur = {}
    hstate = {}
    jstate = {}

    def emit_S(i):
        h, j, kb = blocks[i]
        if h not in hstate:
            K_, V_ = Kh.next(), Vh.next()
            P.dma("sp", [(K_[0:96, :], KT[h])], writes=[K_.b])
            vsrc = VA[:, h * 64:(h + 1) * 64].rearrange("(kb p) d -> p kb d", p=128)
            step = max(1, min(8, NKB))
            P.dma("sp", [(V_[:, a:a + step, 0:64], vsrc[:, a:a + step, :]) for a in range(0, NKB, step)], writes=[V_.b])
            hstate.clear()
            hstate[h] = (K_, V_)
        K_, V_ = hstate[h]
        if (h, j) not in jstate:
            q_ = qT.next()
            P.dma("sp", [(q_[0:96, :], QT[h, :, j * 512:(j + 1) * 512])], writes=[q_.b])
            jstate.clear()
            jstate[(h, j)] = (q_, ps_o.next())
        q_, O_ = jstate[(h, j)]
        m = kb - 4 * j
        lo = 128 * m if m > 0 else 0
        ps = ps_s.next()
        MM(P, ps[:, lo:512], K_[0:96, kb * 128:(kb + 1) * 128], q_[0:96, lo:512], True, True, reads=[K_.b, q_.b], writes=[ps.b])
        pt = PT.next()
        ACTF(P, pt[:, lo:512], ps[:, lo:512], AF.Exp, reads=[ps.b], writes=[pt.b])
        if m >= 0:
            TT(P, "dve", pt[:, lo:lo + 128], pt[:, lo:lo + 128], tri[:], ALU.mult, reads=[pt.b, tri.b], writes=[pt.b])
        cur[i] = (pt, lo, V_, O_)

    def emit_PV(i):
        h, j, kb = blocks[i]
        pt, lo, V_, O_ = cur.pop(i)
        last = 4 * (j + 1) - 1
        MM(P, O_[:, lo:512], V_[:, kb, :], pt[:, lo:512], kb == 0, kb == last, reads=[V_.b, pt.b], writes=[O_.b])
        if kb == last:
            E = ER.next()
            CP(P, "act", E[0:64, :], O_[0:64, :], reads=[O_.b], writes=[E.b])
            P.op("dve", lambda e: e.reciprocal(out=E[64:128, :], in_=O_[64:128, :]), reads=[O_.b], writes=[E.b])
            eh, el = EH.next(), EL.next()
            CP(P, "dve", eh[64:128, :], E[64:128, :], reads=[E.b], writes=[eh.b])
            TT(P, "dve", el[64:128, :], E[64:128, :], eh[64:128, :], ALU.subtract, reads=[E.b, eh.b], writes=[el.b])
            pr = ps_r.next()
            MM(P, pr[:, :], shf[:], eh[:], True, False, reads=[shf.b, eh.b], writes=[pr.b])
            MM(P, pr[:, :], shf[:], el[:], False, True, reads=[shf.b, el.b], writes=[pr.b])
            yb = YB.next()
            TT(P, "dve", yb[0:64, :], E[0:64, :], pr[0:64, :], ALU.mult, reads=[E.b, pr.b], writes=[yb.b])
            P.dma("sp", [(YT[256 + h * 64:256 + (h + 1) * 64, j * 512:(j + 1) * 512], yb[0:64, :])], reads=[yb.b])

    LOOK = 2
    n = len(blocks)
    for i in range(n + LOOK):
        if i < n:
            emit_S(i)
        if i - LOOK >= 0:
            emit_PV(i - LOOK)
    ph.close()


def phase_s5(C, UT, YT, W, l, maskC):
    P, S, nc = C.P, C.S, C.nc
    T = 512
    NCH = S // T
    TWO_PI = 2.0 * math.pi
    ph = Phase(C)
    f8 = lambda nm: ph.tile(nm, [128, 8], F32)
    ar, ai, ldt, dt, mag, th, kf, rr, mm_, sn, cs_, abr, abi, den, am1, cr, ci, tq = [f8(n) for n in
        ("ar", "ai", "ldt", "dt", "mag", "th", "kf", "rr", "mm_", "sn", "cs_", "abr", "abi", "den", "am1", "cr", "ci", "tq")]
    ki = ph.tile("ki", [128, 8], I32)
    with nc.allow_non_contiguous_dma(reason="tiny ssm parameter vectors"):
        a_re_v = W["ssm_a_re"][l].rearrange("(j two) p -> two p j", two=2)
        a_im_v = W["ssm_a_im"][l].rearrange("(j two) p -> two p j", two=2)
        P.dma("sp", [(ar[0:64, :], a_re_v[0]), (ar[64:128, :], a_re_v[1])], writes=[ar.b])
        P.dma("sp", [(ai[0:64, :], a_im_v[0]), (ai[64:128, :], a_im_v[1])], writes=[ai.b])
        ldt_v = W["ssm_log_dt"][l].rearrange("(j two) -> two j", two=2)
        P.dma("sp", [(ldt[0:64, :], ldt_v[0:1, :].broadcast_to([64, 8])), (ldt[64:128, :], ldt_v[1:2, :].broadcast_to([64, 8]))], writes=[ldt.b])
    ACTF(P, dt[:], ldt[:], AF.Exp, reads=[ldt.b], writes=[dt.b])
    TT(P, "dve", tq[:], ar[:], dt[:], ALU.mult, reads=[ar.b, dt.b], writes=[tq.b])
    ACTF(P, mag[:], tq[:], AF.Exp, reads=[tq.b], writes=[mag.b])
    TT(P, "dve", th[:], ai[:], dt[:], ALU.mult, reads=[ai.b, dt.b], writes=[th.b])

    sc_names = ("ang", "kf", "r", "h", "p", "S", "Cp", "sin", "cos")
    sc_t = {"ang": th, "kf": kf, "r": rr, "h": mm_, "p": f8("scp"), "S": f8("scS"), "Cp": f8("scC"), "sin": sn, "cos": cs_}
    emit_sincos(P, sn[:], cs_[:], th[:], kf[:], rr[:], mm_[:], sc_t["p"][:], sc_t["S"][:], sc_t["Cp"][:], {n: sc_t[n].b for n in sc_names})
    TT(P, "dve", abr[:], mag[:], cs_[:], ALU.mult, reads=[mag.b, cs_.b], writes=[abr.b])
    TT(P, "dve", abi[:], mag[:], sn[:], ALU.mult, reads=[mag.b, sn.b], writes=[abi.b])
    TT(P, "dve", den[:], ar[:], ar[:], ALU.mult, reads=[ar.b], writes=[den.b])
    TT(P, "dve", tq[:], ai[:], ai[:], ALU.mult, reads=[ai.b], writes=[tq.b])
    TT(P, "dve", den[:], den[:], tq[:], ALU.add, reads=[den.b, tq.b], writes=[den.b])
    P.op("dve", lambda e: e.reciprocal(out=den[:], in_=den[:]), reads=[den.b], writes=[den.b])
    TS(P, "dve", am1[:], abr[:], -1.0, None, ALU.add, None, reads=[abr.b], writes=[am1.b])
    TT(P, "dve", cr[:], am1[:], ar[:], ALU.mult, reads=[am1.b, ar.b], writes=[cr.b])
    TT(P, "dve", tq[:], abi[:], ai[:], ALU.mult, reads=[abi.b, ai.b], writes=[tq.b])
    TT(P, "dve", cr[:], cr[:], tq[:], ALU.add, reads=[cr.b, tq.b], writes=[cr.b])
    TT(P, "dve", cr[:], cr[:], den[:], ALU.mult, reads=[cr.b, den.b], writes=[cr.b])
    TT(P, "dve", ci[:], abi[:], ar[:], ALU.mult, reads=[abi.b, ar.b], writes=[ci.b])
    TT(P, "dve", tq[:], am1[:], ai[:], ALU.mult, reads=[am1.b, ai.b], writes=[tq.b])
    TT(P, "dve", ci[:], ci[:], tq[:], ALU.subtract, reads=[ci.b, tq.b], writes=[ci.b])
    TT(P, "dve", ci[:], ci[:], den[:], ALU.mult, reads=[ci.b, den.b], writes=[ci.b])

    br = ph.tile("br", [128, 8, 16], F32)
    bi = ph.tile("bi", [128, 8, 16], F32)
    bbr = ph.tile("bbr", [128, 8, 16], F32)
    bbi = ph.tile("bbi", [128, 8, 16], F32)
    tb = ph.tile("tb", [128, 8, 16], F32)
    b_re_v = W["ssm_b_re"][l].rearrange("(j two) p c -> two p j c", two=2)
    b_im_v = W["ssm_b_im"][l].rearrange("(j two) p c -> two p j c", two=2)
    P.dma("sp", [(br[0:64], b_re_v[0]), (br[64:128], b_re_v[1])], writes=[br.b])
    P.dma("sp", [(bi[0:64], b_im_v[0]), (bi[64:128], b_im_v[1])], writes=[bi.b])
    for j in range(8):
        TS(P, "dve", tb[:, j, :], bi[:, j, :], ci[:, j:j + 1], None, ALU.mult, None, reads=[bi.b, ci.b], writes=[tb.b])
        STT(P, "dve", bbr[:, j, :], br[:, j, :], cr[:, j:j + 1], tb[:, j, :], ALU.mult, ALU.subtract, reads=[br.b, cr.b, tb.b], writes=[bbr.b])
        TS(P, "dve", tb[:, j, :], br[:, j, :], ci[:, j:j + 1], None, ALU.mult, None, reads=[br.b, ci.b, bbr.b], writes=[tb.b])
        STT(P, "dve", bbi[:, j, :], bi[:, j, :], cr[:, j:j + 1], tb[:, j, :], ALU.mult, ALU.add, reads=[bi.b, cr.b, tb.b], writes=[bbi.b])
    Mz = ph.tile("Mz", [128, 16, 128], F32)
    BT = ph.tile("BT", [128, 16, 128], BF16)
    P.op("dve", lambda e: e.memset(Mz[:], 0.0), writes=[Mz.b])
    for j in range(8):
        ch0 = (j % 4) * 32
        for ri, bb in enumerate((bbr, bbi)):
            CP(P, "dve", Mz[0:64, 2 * j + ri, ch0:ch0 + 16], bb[0:64, j, :], reads=[bb.b], writes=[Mz.b])
            CP(P, "dve", Mz[64:128, 2 * j + ri, ch0 + 16:ch0 + 32], bb[64:128, j, :], reads=[bb.b], writes=[Mz.b])
    for q in range(4):
        pst = C.ps[q % 2]
        for k in range(4):
            P.op("pe", lambda e: e.transpose(out=pst[:, k * 128:(k + 1) * 128], in_=Mz[:, 4 * q + k, :], identity=C.ident[:]),
                 reads=[Mz.b, C.ident.b], writes=[pst.b])
        CP(P, "dve", BT[:, 4 * q:4 * q + 4, :].rearrange("p a n -> p (a n)"), pst[:, :], reads=[pst.b], writes=[BT.b])
    Cn2 = ph.tile("Cn2", [128, 4, 128], F32)
    CT = ph.tile("CT", [128, 16, 128], BF16)
    mC = ph.tile("mC", [128, 4, 128], F32)
    P.dma("sp", [(mC[:], maskC)], writes=[mC.b])
    for ri, nm in enumerate(("ssm_c_re", "ssm_c_im")):
        cv = W[nm][l].rearrange("(cc gl) c p -> cc (gl c) p", cc=2)
        for cc in range(2):
            P.dma("sp", [(Cn2[:, cc * 2 + ri, 0:64], cv[cc]), (Cn2[:, cc * 2 + ri, 64:128], cv[cc])], writes=[Cn2.b])
    pst = C.ps[2]
    for k in range(4):
        P.op("pe", lambda e: e.transpose(out=pst[:, k * 128:(k + 1) * 128], in_=Cn2[:, k, :], identity=C.ident[:]),
             reads=[Cn2.b, C.ident.b], writes=[pst.b])
    for cc in range(2):
        for j4 in range(4):
            j = cc * 4 + j4
            TT(P, "dve", CT[:, 2 * j, :], pst[:, (cc * 2) * 128:(cc * 2 + 1) * 128], mC[:, j4, :], ALU.mult, reads=[pst.b, mC.b], writes=[CT.b])
            STT(P, "dve", CT[:, 2 * j + 1, :], pst[:, (cc * 2 + 1) * 128:(cc * 2 + 2) * 128], -1.0, mC[:, j4, :], ALU.mult, ALU.mult,
                reads=[pst.b, mC.b], writes=[CT.b])
    dsk = ph.tile("dsk", [128, 2], F32)
    bglu = ph.tile("bglu", [128, 2], F32)
    Wglu = ph.tile("Wglu", [128, 2, 256], BF16)
    on256 = ph.tile("on256", [128, 128], BF16)
    P.op("dve", lambda e: e.memset(on256[:], 1.0 / 256), writes=[on256.b])
    with nc.allow_non_contiguous_dma(reason="tiny ssm parameter vectors"):
        P.dma("sp", [(dsk[:], W["ssm_d"][l].rearrange("g c -> (g c)").rearrange("(cc p) -> p cc", p=128))], writes=[dsk.b])
        P.dma("sp", [(bglu[:], W["ssm_glu_b"][l].rearrange("(cc p) -> p cc", p=128))], writes=[bglu.b])
    nbglu = ph.tile("nbglu", [128, 2], F32)
    TS(P, "dve", nbglu[:], bglu[:], -1.0, None, ALU.mult, None, reads=[bglu.b], writes=[nbglu.b])
    wgs = ph.tile("wgs", [128, 2, 256], F32)
    P.dma("sp", [(wgs[:], W["ssm_glu_w"][l].rearrange("(k p) f -> p k f", p=128))], writes=[wgs.b])
    CP(P, "dve", Wglu[:], wgs[:], reads=[wgs.b], writes=[Wglu.b])
    COS = ph.tile("COS", [128, 8, T], F32)
    SIN = ph.tile("SIN", [128, 8, T], F32)
    tmp = ph.tile("tmpd", [128, 8, T // 2], F32)
    CP(P, "dve", COS[:, :, 0], cs_[:], reads=[cs_.b], writes=[COS.b])
    CP(P, "dve", SIN[:, :, 0], sn[:], reads=[sn.b], writes=[SIN.b])
    w = 1
    while w < T:
        for j in range(8):
            cj = COS[:, j, w - 1:w]
            sj = SIN[:, j, w - 1:w]
            TS(P, "dve", tmp[:, j, 0:w], SIN[:, j, 0:w], sj, None, ALU.mult, None, reads=[SIN.b], writes=[tmp.b])
            STT(P, "dve", COS[:, j, w:2 * w], COS[:, j, 0:w], cj, tmp[:, j, 0:w], ALU.mult, ALU.subtract, reads=[COS.b, tmp.b], writes=[COS.b])
            TS(P, "dve", tmp[:, j, 0:w], SIN[:, j, 0:w], cj, None, ALU.mult, None, reads=[SIN.b, COS.b], writes=[tmp.b])
            STT(P, "dve", SIN[:, j, w:2 * w], COS[:, j, 0:w], sj, tmp[:, j, 0:w], ALU.mult, ALU.add, reads=[COS.b, SIN.b, tmp.b], writes=[SIN.b])
        w *= 2
    if USE_HW_SCAN:
        DEC = ph.tile("DEC", [128, 8, T], F32)
        P.op("dve", lambda e: e.memset(DEC[:], 1.0), writes=[DEC.b])
        for j in range(8):
            TS(P, "dve", DEC[:, j, :], DEC[:, j, :], mag[:, j:j + 1], None, ALU.mult, None, reads=[DEC.b, mag.b], writes=[DEC.b])
    NST = int(math.log2(T))
    RP = ph.tile("RP", [128, NST, 8], F32)
    CP(P, "dve", RP[:, 0, :], mag[:], reads=[mag.b], writes=[RP.b])
    for k in range(1, NST):
        TT(P, "dve", RP[:, k, :], RP[:, k - 1, :], RP[:, k - 1, :], ALU.mult, reads=[RP.b], writes=[RP.b])
    car_re = ph.tile("car_re", [128, 8], F32)
    car_im = ph.tile("car_im", [128, 8], F32)
    car_b = [Buf("car%d" % j) for j in range(8)]
    P.op("dve", lambda e: e.memset(car_re[:], 0.0), writes=[car_re.b])
    P.op("dve", lambda e: e.memset(car_im[:], 0.0), writes=[car_im.b])
    for j in range(8):
        car_b[j].w = car_im.b.w

    uT = ph.ring("uT", 2, [128, 2, T], F32)
    ub = ph.ring("ub", 2, [128, 2, T], BF16)
    A_ = ph.ring("A_", 2, [128, T], F32)
    B_ = ph.ring("B_", 2, [128, T], F32)
    C_ = ph.ring("C_", 2, [128, T], F32)
    D_ = ph.ring("D_", 2, [128, T], F32)
    gre = ph.ring("gre", 2 if USE_HW_SCAN else 0, [128, T], F32)
    gim = ph.ring("gim", 2 if USE_HW_SCAN else 0, [128, T], F32)
    ctmp = ph.ring("ctmp", 8, [128, 1], F32)
    sc0 = ph.ring("sc0", 2, [128, 2, T], F32)
    sc1 = ph.ring("sc1", 2, [128, 2, T], F32)
    H = ph.ring("H", 1, [128, 16, T], BF16)
    ysb = ph.ring("ysb", 1, [128, 2, T], F32)
    g1 = ph.ring("g1", 1, [128, 2, T], F32)
    g2 = ph.ring("g2", 1, [128, 2, T], F32)
    yg = ph.ring("yg", 1, [128, 2, T], F32)
    ygb = ph.ring("ygb", 2, [128, 2, T], BF16)
    sig = ph.ring("sig", 2, [128, T], F32)
    yc = ph.ring("yc", 1, [128, 2, T], F32)
    ysq = ph.ring("ysq", 2, [128, 2, T], BF16)
    rr_ = ph.ring("rr_", 2, [128, T], F32)
    ycn = ph.ring("ycn", 2, [128, 2, T], BF16)
    ps_b = PsRing(C.ps[0:4])
    ps_y = PsRing(C.ps[4:8])

    for c in range(NCH):
        cs = slice(c * T, (c + 1) * T)
        u_ = uT.next()
        ub_ = ub.next()
        P.dma("sp", [(u_[:], UT[:, cs].rearrange("(k p) t -> p k t", p=128))], writes=[u_.b])
        CP(P, "act", ub_[:], u_[:], reads=[u_.b], writes=[ub_.b])
        H_ = H.next()
        for j in range(8):
            cc = j // 4
            p_re = ps_b.next()
            p_im = ps_b.next()
            MM(P, p_re[:, :], BT[:, 2 * j, :], ub_[:, cc, :], True, True, reads=[BT.b, ub_.b], writes=[p_re.b])
            MM(P, p_im[:, :], BT[:, 2 * j + 1, :], ub_[:, cc, :], True, True, reads=[BT.b, ub_.b], writes=[p_im.b])
            a, b, c2, d = A_.next(), B_.next(), C_.next(), D_.next()
            TT(P, "dve", a[:], p_re[:, :], COS[:, j, :], ALU.mult, reads=[p_re.b, COS.b], writes=[a.b])
            TT(P, "dve", b[:], p_im[:, :], SIN[:, j, :], ALU.mult, reads=[p_im.b, SIN.b], writes=[b.b])
            TT(P, "dve", c2[:], p_im[:, :], COS[:, j, :], ALU.mult, reads=[p_im.b, COS.b], writes=[c2.b])
            TT(P, "dve", d[:], p_re[:, :], SIN[:, j, :], ALU.mult, reads=[p_re.b, SIN.b], writes=[d.b])
            TT(P, "dve", a[:], a[:], b[:], ALU.add, reads=[a.b, b.b], writes=[a.b])
            TT(P, "dve", c2[:], c2[:], d[:], ALU.subtract, reads=[c2.b, d.b], writes=[c2.b])
            if USE_HW_SCAN:
                gr_t, gi_t = gre.next(), gim.next()
                gr, gi = gr_t[:], gi_t[:]
                gr_b, gi_b = gr_t.b, gi_t.b
                dec = DEC[:, j, :]
                P.op("dve", lambda e: e.tensor_tensor_scan(out=gr, data0=dec, data1=a[:], initial=car_re[:, j:j + 1], op0=ALU.mult, op1=ALU.add),
                     reads=[DEC.b, a.b, car_b[j]], writes=[gr_b])
                P.op("dve", lambda e: e.tensor_tensor_scan(out=gi, data0=dec, data1=c2[:], initial=car_im[:, j:j + 1], op0=ALU.mult, op1=ALU.add),
                     reads=[DEC.b, c2.b, car_b[j]], writes=[gi_b])
            else:
                s0, s1_ = sc0.next(), sc1.next()
                STT(P, "dve", s0[:, 0, 0:1], car_re[:, j:j + 1], mag[:, j:j + 1], a[:, 0:1], ALU.mult, ALU.add, reads=[car_b[j], mag.b, a.b], writes=[s0.b])
                STT(P, "dve", s0[:, 1, 0:1], car_im[:, j:j + 1], mag[:, j:j + 1], c2[:, 0:1], ALU.mult, ALU.add, reads=[car_b[j], mag.b, c2.b], writes=[s0.b])
                CP(P, "dve", s0[:, 0, 1:T], a[:, 1:T], reads=[a.b], writes=[s0.b])
                CP(P, "dve", s0[:, 1, 1:T], c2[:, 1:T], reads=[c2.b], writes=[s0.b])
                cur, nxt = s0, s1_
                for k in range(NST):
                    w_ = 1 << k
                    CP(P, "act", nxt[:, :, 0:w_], cur[:, :, 0:w_], reads=[cur.b], writes=[nxt.b])
                    STT(P, "dve", nxt[:, :, w_:T], cur[:, :, 0:T - w_], RP[:, k, j:j + 1], cur[:, :, w_:T], ALU.mult, ALU.add,
                        reads=[cur.b, RP.b], writes=[nxt.b])
                    cur, nxt = nxt, cur
                gr, gi = cur[:, 0, :], cur[:, 1, :]
                gr_b = gi_b = cur.b
            ct, ct1 = ctmp.next(), ctmp.next()
            TS(P, "dve", ct[:], gi[:, T - 1:T], SIN[:, j, T - 1:T], None, ALU.mult, None, reads=[gi_b, SIN.b], writes=[ct.b])
            TS(P, "dve", ct1[:], gr[:, T - 1:T], COS[:, j, T - 1:T], None, ALU.mult, None, reads=[gr_b, COS.b], writes=[ct1.b])
            ct2, ct3 = ctmp.next(), ctmp.next()
            TS(P, "dve", ct2[:], gi[:, T - 1:T], COS[:, j, T - 1:T], None, ALU.mult, None, reads=[gi_b, COS.b], writes=[ct2.b])
            TS(P, "dve", ct3[:], gr[:, T - 1:T], SIN[:, j, T - 1:T], None, ALU.mult, None, reads=[gr_b, SIN.b], writes=[ct3.b])
            TT(P, "dve", car_re[:, j:j + 1], ct1[:], ct[:], ALU.subtract, reads=[ct1.b, ct.b], writes=[car_b[j]])
            TT(P, "dve", car_im[:, j:j + 1], ct3[:], ct2[:], ALU.add, reads=[ct3.b, ct2.b], writes=[car_b[j]])
            TT(P, "dve", a[:], gr, COS[:, j, :], ALU.mult, reads=[gr_b, COS.b], writes=[a.b])
            TT(P, "dve", b[:], gi, SIN[:, j, :], ALU.mult, reads=[gi_b, SIN.b], writes=[b.b])
            TT(P, "dve", c2[:], gr, SIN[:, j, :], ALU.mult, reads=[gr_b, SIN.b], writes=[c2.b])
            TT(P, "dve", d[:], gi, COS[:, j, :], ALU.mult, reads=[gi_b, COS.b], writes=[d.b])
            TT(P, "dve", H_[:, 2 * j, :], a[:], b[:], ALU.subtract, reads=[a.b, b.b], writes=[H_.b])
            TT(P, "dve", H_[:, 2 * j + 1, :], c2[:], d[:], ALU.add, reads=[c2.b, d.b], writes=[H_.b])
        y_ = ysb.next()
        for cc in range(2):
            py = ps_y.next()
            for k in range(8):
                MM(P, py[:, :], CT[:, 8 * cc + k, :], H_[:, 8 * cc + k, :], k == 0, k == 7, reads=[CT.b, H_.b], writes=[py.b])
            STT(P, "dve", y_[:, cc, :], u_[:, cc, :], dsk[:, cc:cc + 1], py[:, :], ALU.mult, ALU.add, reads=[u_.b, dsk.b, py.b], writes=[y_.b])
        a1, a2, yg_, ygb_ = g1.next(), g2.next(), yg.next(), ygb.next()
        emit_gelu(C, yg_[:], y_[:], None, a1[:], a2[:], [y_.b], [yg_.b], None, a1.b, a2.b)
        CP(P, "act", ygb_[:], yg_[:], reads=[yg_.b], writes=[ygb_.b])
        yc_ = yc.next()
        for co in range(2):
            pg = ps_y.next()
            for k in range(2):
                MM(P, pg[:, :], Wglu[:, k, co * 128:(co + 1) * 128], ygb_[:, k, :], k == 0, k == 1, reads=[Wglu.b, ygb_.b], writes=[pg.b])
            sg = sig.next()
            TS(P, "dve", sg[:], pg[:, :], bglu[:, co:co + 1], None, ALU.add, None, reads=[pg.b, bglu.b], writes=[sg.b])
            TS(P, "dve", sg[:], sg[:], -43.0, None, ALU.max, None, reads=[sg.b], writes=[sg.b])
            emit_sigmoid(C, sg[:], sg[:], [sg.b], [sg.b])
            TT(P, "dve", yc_[:, co, :], yg_[:, co, :], sg[:], ALU.mult, reads=[yg_.b, sg.b], writes=[yc_.b])
        sq_ = ysq.next()
        TT(P, "dve", sq_[:], yc_[:], yc_[:], ALU.mult, reads=[yc_.b], writes=[sq_.b])
        pr = ps_y.next()
        for k in range(2):
            MM(P, pr[:, :], on256[:], sq_[:, k, :], k == 0, k == 1, reads=[on256.b, sq_.b], writes=[pr.b])
        r_ = rr_.next()
        emit_rsqrt(C, r_[:], pr[:, :], 1.0, C.eps_rms, [pr.b], [r_.b])
        yn = ycn.next()
        for cc in range(2):
            TT(P, "dve", yn[:, cc, :], yc_[:, cc, :], r_[:], ALU.mult, reads=[yc_.b, r_.b], writes=[yn.b])
        P.dma("sp", [(YT[768:1024, cs].rearrange("(k p) t -> p k t", p=128), yn[:])], reads=[yn.b])
    ph.close()


def phase_mix(C, x_in, x_out, YT, W, l):
    P, S, nc = C.P, C.S, C.nc
    G = 512
    NG = S // G
    ph = Phase(C)
    Wo = ph.tile("Wo", [128, 8, D], BF16)
    stg = ph.ring("stgo", 2, [128, D], F32)
    gm = ph.tile("gm", [128, 8], F32)
    with nc.allow_non_contiguous_dma(reason="tiny gain vector"):
        P.dma("sp", [(gm[:], W["mix_norm_g"][l].rearrange("(k p) -> p k", p=128))], writes=[gm.b])
    wo_v = W["w_out"][l].rearrange("(k p) d -> p k d", p=128)
    for k in range(8):
        s_ = stg.next()
        P.dma("sp", [(s_[:], wo_v[:, k, :])], writes=[s_.b])
        TS(P, "dve", Wo[:, k, :], s_[:], gm[:, k:k + 1], None, ALU.mult, None, reads=[s_.b, gm.b], writes=[Wo.b])
    Gbc = ph.tile("Gbc", [128, D], F32)
    Bbc = ph.tile("Bbc", [128, D], F32)
    P.dma("sp", [(Gbc[:], W["ln_g"][l, 1].rearrange("(o d) -> o d", o=1).broadcast_to([128, D]))], writes=[Gbc.b])
    P.dma("sp", [(Bbc[:], W["ln_b"][l, 1].rearrange("(o d) -> o d", o=1).broadcast_to([128, D]))], writes=[Bbc.b])
    on512 = ph.tile("on512", [128, 128], BF16)
    P.op("dve", lambda e: e.memset(on512[:], 1.0 / 512), writes=[on512.b])
    Yt = ph.ring("Yt", 2, [128, 8, G], BF16)
    ysq = ph.ring("ysq", 2, [128, 4, G], BF16)
    rb = ph.ring("rb", 2, [128, G], F32)
    XR = ph.ring("XR", 3, [128, D], F32)
    ZR = ph.ring("ZR", 2, [128, D], F32)
    st6 = ph.ring("st6", 2, [128, 2, 6], F32)
    mv = ph.ring("mv", 2, [128, 2], F32)
    rstd = ph.ring("rstd", 2, [128, 1], F32)
    ps_r = PsRing(C.ps[0:2])
    ps_o = PsRing(C.ps[2:8])
    for g in range(NG):
        cs = slice(g * G, (g + 1) * G)
        Y = Yt.next()
        P.dma("sp", [(Y[:], YT[:, cs].rearrange("(k p) t -> p k t", p=128))], writes=[Y.b])
        sq_ = ysq.next()
        TT(P, "dve", sq_[:], Y[:, 2:6, :], Y[:, 2:6, :], ALU.mult, reads=[Y.b], writes=[sq_.b])
        pr = ps_r.next()
        for k in range(4):
            MM(P, pr[:, :], on512[:], sq_[:, k, :], k == 0, k == 3, reads=[on512.b, sq_.b], writes=[pr.b])
        r_ = rb.next()
        emit_rsqrt(C, r_[:], pr[:, :], 1.0, C.eps_rms, [pr.b], [r_.b])
        for k in range(2, 6):
            TT(P, "dve", Y[:, k, :], Y[:, k, :], r_[:], ALU.mult, reads=[Y.b, r_.b], writes=[Y.b])
        for s in range(4):
            r0 = g * G + s * 128
            X = XR.next()
            P.dma("sp", [(X[:], x_in[r0:r0 + 128, :])], writes=[X.b])
            Z = ZR.next()
            for dh in range(2):
                po = ps_o.next()
                for k in range(8):
                    MM(P, po[:, :], Y[:, k, s * 128:(s + 1) * 128], Wo[:, k, dh * 512:(dh + 1) * 512], k == 0, k == 7, reads=[Y.b, Wo.b], writes=[po.b])
                CP(P, "act", Z[:, dh * 512:(dh + 1) * 512], po[:, :], reads=[po.b], writes=[Z.b])
            STT(P, "dve", Z[:], X[:], ALPHA, Z[:], ALU.mult, ALU.add, reads=[X.b, Z.b], writes=[Z.b])
            emit_ln_epilogue(C, Z, mv.next(), st6.next(), rstd.next(), Gbc, Bbc)
            P.dma("sp", [(x_out[r0:r0 + 128, :], Z[:])], reads=[Z.b], owner=Z.b)
    ph.close()


WEIGHT_NAMES = ["ln_g", "ln_b", "ffn1_w_gate", "ffn1_w_up", "ffn1_w_down", "w_in", "gmlp_norm_g", "gmlp_ws", "gmlp_bs",
                "mla_q_norm_g", "mla_w_uq", "mla_kv_norm_g", "mla_w_ukv", "ssm_a_re", "ssm_a_im", "ssm_b_re", "ssm_b_im",
                "ssm_c_re", "ssm_c_im", "ssm_d", "ssm_log_dt", "ssm_glu_w", "ssm_glu_b", "mix_norm_g", "w_out",
                "ffn2_w_gate", "ffn2_w_up", "ffn2_w_down"]


def host_consts():
    inv = (1.0 / (10000.0 ** (np.arange(0, 32, 2, dtype=np.float32) / 32.0))).astype(np.float32)
    shiftm = np.zeros((128, 128), np.float32)
    shiftm[np.arange(64) + 64, np.arange(64)] = 1
    maskC = np.zeros((128, 4, 128), np.float32)
    for j4 in range(4):
        for n in range(128):
            c0 = j4 * 32 + (n // 64) * 16
            maskC[n, j4, c0:c0 + 16] = 1
    return {
        "c_ident": np.eye(128, dtype=np.float32),
        "c_tril": np.tril(np.ones((128, 128), np.float32)),
        "c_triu": np.triu(np.ones((128, 128), np.float32)),
        "c_shift": shiftm,
        "c_maskC": maskC,
        "c_invf": np.concatenate([inv, inv])[:, None].astype(np.float32),
    }


def build_program(S, wshapes, depth=DEPTH, phases="ABCDEF"):
    nc = bass.Bass("TRN2", target_bir_lowering=False)
    x = nc.dram_tensor("x", [S, D], F32, kind="ExternalInput").ap()
    pos = nc.dram_tensor("positions", [S], I32, kind="ExternalInput").ap()
    W = {k: nc.dram_tensor(k, list(wshapes[k]), F32, kind="ExternalInput").ap() for k in WEIGHT_NAMES}
    hc = host_consts()
    Cn = {k: nc.dram_tensor(k, list(v.shape), F32, kind="ExternalInput").ap() for k, v in hc.items()}
    y = nc.dram_tensor("y", [S, D], F32, kind="ExternalOutput").ap()
    XS = nc.dram_tensor("XS", [S, D], F32).ap()
    ROT = nc.dram_tensor("ROT", [2, 32, S], F32).ap()
    YT = nc.dram_tensor("YT", [1024, S], BF16).ap()
    QT = nc.dram_tensor("QT", [NH, 96, S], BF16).ap()
    KT = nc.dram_tensor("KT", [NH, 96, S], BF16).ap()
    VA = nc.dram_tensor("VA", [S, 512], BF16).ap()
    UT = nc.dram_tensor("UT", [256, S], F32).ap()
    with ExitStack() as st:
        C = Ctx(nc, st, S)
        st.enter_context(nc.Block())
        C.load_consts(Cn["c_ident"])
        phase_rope_tables(C, pos, Cn["c_invf"], ROT)
        for l in range(depth):
            src = x if l == 0 else XS
            phase_ffn(C, src, XS, "xa", "xb", W["ffn1_w_gate"][l], W["ffn1_w_up"][l], W["ffn1_w_down"][l], W["ln_g"][l, 0], W["ln_b"][l, 0])
            phase_proj(C, XS, W, l, ROT, YT, QT, KT, VA, UT, Cn["c_tril"])
            phase_attn(C, QT, KT, VA, YT, Cn["c_triu"], Cn["c_shift"])
            phase_s5(C, UT, YT, W, l, Cn["c_maskC"])
            phase_mix(C, XS, XS, YT, W, l)
            dst = y if l == depth - 1 else XS
            phase_ffn(C, XS, dst, "xc", "xd", W["ffn2_w_gate"][l], W["ffn2_w_up"][l], W["ffn2_w_down"][l], W["ln_g"][l, 2], W["ln_b"][l, 2])
        C.P.barrier()
        C.ninst = C.P.ninst
    return nc


_CACHE = {}


def kernel(**inputs):
    x = np.ascontiguousarray(np.asarray(inputs["x"], dtype=np.float32))
    B, S, _ = x.shape
    pos = np.ascontiguousarray(np.asarray(inputs["positions"], dtype=np.int32))
    wts = {k: np.ascontiguousarray(np.asarray(inputs[k], dtype=np.float32)) for k in WEIGHT_NAMES}
    key = (S,)
    if key not in _CACHE:
        _CACHE[key] = build_program(S, {k: v.shape for k, v in wts.items()})
    nc = _CACHE[key]
    hc = host_consts()
    in_maps = []
    for b in range(B):
        m = {"x": x[b], "positions": pos[b]}
        m.update(wts)
        m.update(hc)
        in_maps.append(m)
    res = run_bass_kernel_spmd(nc, in_maps, core_ids=list(range(B)))
    return np.stack([np.asarray(r["y"], dtype=np.float32) for r in res.results], axis=0)
```

```python
import math
from contextlib import ExitStack

import numpy as np
import concourse.bass as bass
import concourse.mybir as mybir
from concourse.bass_utils import run_bass_kernel_spmd

F32 = mybir.dt.float32
BF16 = mybir.dt.bfloat16
I32 = mybir.dt.int32
AF = mybir.ActivationFunctionType
ALU = mybir.AluOpType
AX = mybir.AxisListType

D = 1024
DFF = 2816
NFC = DFF // 128
DEPTH = 4
ALPHA = (2 * DEPTH) ** 0.25
LN_EPS = 1e-5
RMS_EPS = 1e-6
GM_W = 256
QL = 256
KVL = 128
ROPE = 32
SSM_W = 256
O1 = 2 * GM_W
O2 = O1 + QL
O3 = O2 + KVL
O4 = O3 + ROPE
IN_COLS = O4 + SSM_W
NH = 8
HD = 96
SM_SCALE = HD ** -0.5
GELU_C = 0.044715
GELU_S = 2.0 * math.sqrt(2.0 / math.pi)
INTERLEAVE_CD = True
USE_HW_SCAN = False
PI_IN = 3.1415925


class Buf:
    __slots__ = ("name", "w", "r", "dsem", "dcnt")

    def __init__(self, name):
        self.name = name
        self.w = None
        self.r = {}
        self.dsem = None
        self.dcnt = 0


class Prog:
    def __init__(self, nc, stack):
        self.nc = nc
        self.stack = stack
        self.eng = {"pe": nc.tensor, "act": nc.scalar, "dve": nc.vector, "pool": nc.gpsimd, "sp": nc.sync}
        self.sem = {}
        self.cnt = {}
        for k in self.eng:
            self.sem[k] = stack.enter_context(nc.semaphore("s_" + k))
            self.cnt[k] = 0
        self.waited = {}
        self.nsem = 5
        self.ninst = 0
        self.free_dsems = {}

    def _wait(self, e, toks):
        best = {}
        for t in toks:
            if t is None:
                continue
            sem, val, src = t
            if src == "pe" and e == "pe":
                continue
            k = id(sem)
            if k not in best or best[k][1] < val:
                best[k] = t
        for k, (sem, val, src) in best.items():
            wk = (e, k)
            if self.waited.get(wk, -1) >= val:
                continue
            self.eng[e].wait_ge(sem, val)
            self.waited[wk] = val

    def _deps(self, reads, writes):
        toks = []
        for b in reads:
            toks.append(b.w)
        for b in writes:
            toks.append(b.w)
            toks.extend(b.r.values())
        return toks

    def _mark(self, tok, reads, writes):
        k = id(tok[0])
        for b in reads:
            old = b.r.get(k)
            if old is None or old[1] < tok[1]:
                b.r[k] = tok
        for b in writes:
            b.w = tok
            b.r = {}

    def op(self, e, fn, reads=(), writes=()):
        self._wait(e, self._deps(reads, writes))
        ins = fn(self.eng[e])
        self.cnt[e] += 1
        ins.then_inc(self.sem[e], 1)
        tok = (self.sem[e], self.cnt[e], e)
        self._mark(tok, reads, writes)
        self.ninst += 1
        return tok

    def _get_dsem(self, b):
        if b.dsem is None:
            b.dsem = self.stack.enter_context(self.nc.semaphore("d%d" % self.nsem))
            self.nsem += 1
        return b.dsem

    def dma(self, q, pairs, reads=(), writes=(), owner=None):
        if owner is None:
            owner = writes[0] if writes else reads[0]
        self._wait(q, self._deps(reads, writes))
        sem = self._get_dsem(owner)
        for (o, i) in pairs:
            self.eng[q].dma_start(out=o, in_=i).then_inc(sem, 16)
            owner.dcnt += 16
            self.ninst += 1
        tok = (sem, owner.dcnt, "dma")
        self._mark(tok, reads, writes)
        return tok

    def finish(self, bufs):
        toks = []
        for b in bufs:
            toks.append(b.w)
            toks.extend(b.r.values())
        self._wait("sp", toks)
        self.eng["sp"].nop() if hasattr(self.eng["sp"], "nop") else None


class Tile:
    def __init__(self, P, name, shape, dt, psum=False):
        nc = P.nc
        if psum:
            self.t = P.stack.enter_context(nc.psum_tensor("sb_" + name, shape, dt))
        else:
            self.t = P.stack.enter_context(nc.sbuf_tensor("sb_" + name, shape, dt))
        self.b = Buf(name)
        self.shape = shape

    def __getitem__(self, idx):
        return self.t[idx]


class Ring:
    def __init__(self, P, name, n, shape, dt, psum=False):
        self.tiles = [Tile(P, "%s%d" % (name, i), shape, dt, psum) for i in range(n)]
        self.i = 0

    def next(self):
        t = self.tiles[self.i % len(self.tiles)]
        self.i += 1
        return t


class DramRegions:
    def __init__(self):
        self.d = {}

    def get(self, *key):
        b = self.d.get(key)
        if b is None:
            b = Buf("dram_" + "_".join(str(k) for k in key))
            self.d[key] = b
        return b


class Ctx:
    def __init__(self, nc, stack, S):
        self.nc = nc
        self.S = S
        self.P = Prog(nc, stack)
        self.stack = stack
        self.dr = DramRegions()
        self.uid = 0
        P = self.P
        self.ps = [Tile(P, "psb%d" % i, [128, 512], F32, psum=True) for i in range(8)]
        self.ident = Tile(P, "ident", [128, 128], F32)
        self.eps_ln = Tile(P, "eps_ln", [128, 1], F32)
        self.eps_rms = Tile(P, "eps_rms", [128, 1], F32)
        self.one = Tile(P, "one_c", [128, 1], F32)

    def load_consts(self, ident_ap):
        P = self.P
        P.dma("sp", [(self.ident[:], ident_ap)], writes=[self.ident.b])
        P.op("dve", lambda e: e.memset(self.eps_ln[:], LN_EPS), writes=[self.eps_ln.b])
        P.op("dve", lambda e: e.memset(self.eps_rms[:], RMS_EPS), writes=[self.eps_rms.b])
        P.op("dve", lambda e: e.memset(self.one[:], 1.0), writes=[self.one.b])


class PsRing:
    def __init__(self, tiles):
        self.tiles = tiles
        self.i = 0

    def next(self):
        t = self.tiles[self.i % len(self.tiles)]
        self.i += 1
        return t


class Phase:
    def __init__(self, C):
        self.C = C
        self.st = ExitStack()
        self.tiles = []

    def tile(self, name, shape, dt):
        P = self.C.P
        t = Tile.__new__(Tile)
        self.C.uid += 1
        t.t = self.st.enter_context(self.C.nc.sbuf_tensor("sb%d_%s" % (self.C.uid, name), shape, dt))
        t.b = Buf(name)
        t.shape = shape
        self.tiles.append(t)
        return t

    def ring(self, name, n, shape, dt):
        r = Ring.__new__(Ring)
        r.tiles = [self.tile("%s%d" % (name, i), shape, dt) for i in range(n)]
        r.i = 0
        return r

    def close(self):
        self.C.P.barrier()
        for t in self.tiles:
            self.C.P.release(t.b)
        self.st.close()


def _barrier(self):
    toks = [(self.sem[k], self.cnt[k], k) for k in self.eng if self.cnt[k] > 0]
    toks += [(s, c[0], "dma") for (s, c) in self.all_dsems]
    for e in self.eng:
        for (sem, val, src) in toks:
            if src == e:
                continue
            wk = (e, id(sem))
            if self.waited.get(wk, -1) >= val:
                continue
            self.eng[e].wait_ge(sem, val)
            self.waited[wk] = val


def _release(self, b):
    if b.dsem is not None:
        for kind, ds in b.dsem.items():
            self.free_dsems.setdefault(kind, []).append(ds)
        b.dsem = None


def _get_dsem2(self, b, kind):
    if b.dsem is None:
        b.dsem = {}
    if kind not in b.dsem:
        pool = self.free_dsems.setdefault(kind, [])
        if pool:
            b.dsem[kind] = pool.pop()
        else:
            sem = self.stack.enter_context(self.nc.semaphore("d%s%d" % (kind, self.nsem)))
            self.nsem += 1
            b.dsem[kind] = (sem, [0])
            self.all_dsems.append(b.dsem[kind])
    return b.dsem[kind]


def _dma2(self, q, pairs, reads=(), writes=(), owner=None):
    if owner is None:
        owner = writes[0] if writes else reads[0]
    self._wait(q, self._deps(reads, writes))
    sem, c = self._get_dsem(owner, "sw" if q == "pool" else "hw")
    for (o, i) in pairs:
        self.eng[q].dma_start(out=o, in_=i).then_inc(sem, 16)
        c[0] += 16
        self.ninst += 1
    tok = (sem, c[0], "dma")
    self._mark(tok, reads, writes)
    return tok


def _finish2(self, bufs):
    toks = []
    for b in bufs:
        toks.append(b.w)
        toks.extend(b.r.values())
    self._wait("sp", toks)


Prog.barrier = _barrier
Prog.release = _release
Prog._get_dsem = _get_dsem2
Prog.dma = _dma2
Prog.finish = _finish2
_old_init = Prog.__init__


def _init2(self, nc, stack):
    _old_init(self, nc, stack)
    self.all_dsems = []


Prog.__init__ = _init2


def emit_transpose_x(C, X, xT, xT_b, sub, psring, ncols=1024):
    P = C.P
    ndc = ncols // 128
    for q in range(0, ndc, 4):
        ps = psring.next()
        n = min(4, ndc - q)
        for j in range(n):
            dc = q + j
            P.op("pe", lambda e: e.transpose(out=ps[:, j * 128:(j + 1) * 128], in_=X[:, dc * 128:(dc + 1) * 128], identity=C.ident[:]),
                 reads=[X.b, C.ident.b], writes=[ps.b])
        src = ps[:, 0:n * 128].rearrange("p (j t) -> p j t", j=n)
        dst = xT[:, q:q + n, sub * 128:(sub + 1) * 128]
        eng = "act" if (q // 4) % 2 == 0 else "dve"
        if eng == "act":
            P.op("act", lambda e: e.copy(out=dst, in_=src), reads=[ps.b], writes=[xT_b])
        else:
            P.op("dve", lambda e: e.tensor_copy(out=dst, in_=src), reads=[ps.b], writes=[xT_b])


def emit_ln_epilogue(C, Z, mv, st6, rstd, Gbc, Bbc):
    P = C.P
    for h in range(2):
        P.op("dve", lambda e: e.bn_stats(out=st6[:, h, :], in_=Z[:, h * 512:(h + 1) * 512]), reads=[Z.b], writes=[st6.b])
    P.op("dve", lambda e: e.bn_aggr(out=mv[:], in_=st6[:]), reads=[st6.b], writes=[mv.b])
    emit_rsqrt(C, rstd[:], mv[:, 1:2], 1.0, C.eps_ln, [mv.b], [rstd.b])
    P.op("dve", lambda e: e.tensor_scalar(out=Z[:], in0=Z[:], scalar1=mv[:, 0:1], scalar2=rstd[:, 0:1], op0=ALU.subtract, op1=ALU.mult),
         reads=[Z.b, mv.b, rstd.b], writes=[Z.b])
    P.op("dve", lambda e: e.tensor_tensor(out=Z[:], in0=Z[:], in1=Gbc[:], op=ALU.mult), reads=[Z.b, Gbc.b], writes=[Z.b])
    P.op("dve", lambda e: e.tensor_tensor(out=Z[:], in0=Z[:], in1=Bbc[:], op=ALU.add), reads=[Z.b, Bbc.b], writes=[Z.b])


def phase_ffn(C, x_in, x_out, xin_key, xout_key, wg, wu, wd, lng, lnb):
    P, S = C.P, C.S
    G = min(1024, S)
    NG = S // G
    NSUB = G // 128
    NH2 = G // 512
    ph = Phase(C)
    Wd = ph.tile("Wd", [128, NFC, D], BF16)
    WG = ph.ring("WG", 2, [128, 8, 256], BF16)
    WU = ph.ring("WU", 2, [128, 8, 256], BF16)
    WST = ph.ring("WST", 2, [128, 8, 256], F32)
    XR = ph.ring("XR", 2, [128, D], F32)
    X2 = ph.ring("X2", 2, [128, D], F32)
    ZR = ph.ring("ZR", 2, [128, D], F32)
    xT = ph.tile("xT", [128, 8, G], BF16)
    xT_b = [Buf("xT%d" % i) for i in range(NSUB)]
    gT = ph.tile("gT", [128, NFC, G], BF16)
    gT_b = [Buf("gT%d" % i) for i in range(NFC)]
    SG = ph.ring("SG", 2, [128, 512], F32)
    Gbc = ph.tile("Gbc", [128, D], F32)
    Bbc = ph.tile("Bbc", [128, D], F32)
    st6 = ph.ring("st6", 2, [128, 2, 6], F32)
    mv = ph.ring("mv", 2, [128, 2], F32)
    rstd = ph.ring("rstd", 2, [128, 1], F32)
    ps_t = PsRing(C.ps[0:2])
    ps_gu = PsRing(C.ps[2:6])
    ps_d = PsRing(C.ps[6:8])

    wd_v = wd.rearrange("(c p) d -> p c d", p=128)
    Wd_b = [Buf("Wd%d" % i) for i in range(NFC)]
    for q in range(0, NFC, 2):
        load_cast(P, Wd[:, q:q + 2, :], Wd_b[q], wd_v[:, q:q + 2, :], WST, "pool" if (q // 2) % 2 == 0 else "dve",
                  view=lambda t: t[:].rearrange("p a f -> p (a f)").rearrange("p (c d) -> p c d", c=2))
        Wd_b[q + 1] = Wd_b[q]
    P.dma("sp", [(Gbc[:], lng.rearrange("(o d) -> o d", o=1).broadcast_to([128, D]))], writes=[Gbc.b])
    P.dma("sp", [(Bbc[:], lnb.rearrange("(o d) -> o d", o=1).broadcast_to([128, D]))], writes=[Bbc.b])
    wg_v = wg.rearrange("(c p) f -> p c f", p=128)
    wu_v = wu.rearrange("(c p) f -> p c f", p=128)

    for g in range(NG):
        t0 = g * G
        for s in range(NSUB):
            X = XR.next()
            r0 = t0 + s * 128
            P.dma("sp", [(X[:], x_in[r0:r0 + 128, :])], reads=[C.dr.get(xin_key, r0)], writes=[X.b])
            emit_transpose_x(C, X, xT, xT_b[s], s, ps_t)
        for fc2 in range(NFC // 2):
            wgt = WG.next()
            wut = WU.next()
            load_cast(P, wgt[:], wgt.b, wg_v[:, :, fc2 * 256:(fc2 + 1) * 256], WST, "pool")
            load_cast(P, wut[:], wut.b, wu_v[:, :, fc2 * 256:(fc2 + 1) * 256], WST, "pool")
            for sf in range(2):
                fc = fc2 * 2 + sf
                for h in range(NH2):
                    pg = ps_gu.next()
                    pu = ps_gu.next()
                    xb = xT_b[h * 4:(h + 1) * 4]
                    for (pt, wt) in ((pg, wgt), (pu, wut)):
                        for dc in range(8):
                            P.op("pe", lambda e: e.matmul(pt[:, :], lhsT=wt[:, dc, sf * 128:(sf + 1) * 128], rhs=xT[:, dc, h * 512:(h + 1) * 512],
                                                          start=(dc == 0), stop=(dc == 7)),
                                 reads=[wt.b] + xb, writes=[pt.b])
                    sg = SG.next()
                    emit_sigmoid(C, sg[:], pg[:, :], [pg.b], [sg.b])
                    TT(P, "dve", sg[:], sg[:], pg[:, :], ALU.mult, reads=[sg.b, pg.b], writes=[sg.b])
                    TT(P, "dve", gT[:, fc, h * 512:(h + 1) * 512], sg[:], pu[:, :], ALU.mult, reads=[sg.b, pu.b], writes=[gT_b[fc]])
        for s in range(NSUB):
            r0 = t0 + s * 128
            X = X2.next()
            P.dma("sp", [(X[:], x_in[r0:r0 + 128, :])], reads=[C.dr.get(xin_key, r0)], writes=[X.b])
            Z = ZR.next()
            for dh in range(2):
                pd = ps_d.next()
                for fc in range(NFC):
                    P.op("pe", lambda e: e.matmul(pd[:, :], lhsT=gT[:, fc, s * 128:(s + 1) * 128], rhs=Wd[:, fc, dh * 512:(dh + 1) * 512],
                                                  start=(fc == 0), stop=(fc == NFC - 1)),
                         reads=[gT_b[fc], Wd_b[fc]], writes=[pd.b])
                P.op("act", lambda e: e.activation(out=Z[:, dh * 512:(dh + 1) * 512], in_=pd[:, :], func=AF.Copy, scale=0.5),
                     reads=[pd.b], writes=[Z.b])
            P.op("dve", lambda e: e.scalar_tensor_tensor(out=Z[:], in0=X[:], scalar=ALPHA, in1=Z[:], op0=ALU.mult, op1=ALU.add),
                 reads=[X.b, Z.b], writes=[Z.b])
            emit_ln_epilogue(C, Z, mv.next(), st6.next(), rstd.next(), Gbc, Bbc)
            P.dma("sp", [(x_out[r0:r0 + 128, :], Z[:])], reads=[Z.b], writes=[C.dr.get(xout_key, r0)], owner=Z.b)
    ph.close()


def TT(P, eng, out, in0, in1, op, reads, writes):
    return P.op(eng, lambda e: e.tensor_tensor(out=out, in0=in0, in1=in1, op=op), reads=reads, writes=writes)


def TS(P, eng, out, in0, s1, s2, op0, op1, reads, writes):
    if op1 is None:
        return P.op(eng, lambda e: e.tensor_scalar(out=out, in0=in0, scalar1=s1, scalar2=None, op0=op0), reads=reads, writes=writes)
    return P.op(eng, lambda e: e.tensor_scalar(out=out, in0=in0, scalar1=s1, scalar2=s2, op0=op0, op1=op1), reads=reads, writes=writes)


def STT(P, eng, out, in0, scalar, in1, op0, op1, reads, writes):
    return P.op(eng, lambda e: e.scalar_tensor_tensor(out=out, in0=in0, scalar=scalar, in1=in1, op0=op0, op1=op1), reads=reads, writes=writes)


def ACTF(P, out, in_, func, reads, writes, bias=None, scale=None, accum_out=None):
    kw = {}
    if bias is not None:
        kw["bias"] = bias
    if scale is not None:
        kw["scale"] = scale
    if accum_out is not None:
        kw["accum_out"] = accum_out
    return P.op("act", lambda e: e.activation(out=out, in_=in_, func=func, **kw), reads=reads, writes=writes)


def CP(P, eng, out, in_, reads, writes):
    if eng == "act":
        return P.op("act", lambda e: e.copy(out=out, in_=in_), reads=reads, writes=writes)
    return P.op(eng, lambda e: e.tensor_copy(out=out, in_=in_), reads=reads, writes=writes)


def MM(P, out, lhsT, rhs, start, stop, reads, writes):
    return P.op("pe", lambda e: e.matmul(out, lhsT=lhsT, rhs=rhs, start=start, stop=stop), reads=reads, writes=writes)


def load_cast(P, dst, dst_b, src, stage_ring, eng, view=None):
    s_ = stage_ring.next()
    sv = s_[:] if view is None else view(s_)
    P.dma("sp", [(sv, src)], writes=[s_.b])
    CP(P, eng, dst, sv, reads=[s_.b], writes=[dst_b])


def emit_rsqrt(C, out, in_, scale, eps_tile, reads, writes):
    P = C.P
    ACTF(P, out, in_, AF.Ln, reads=reads + [eps_tile.b], writes=writes, bias=eps_tile[:], scale=scale)
    ACTF(P, out, out, AF.Exp, reads=writes, writes=writes, scale=-0.5)


def emit_sigmoid(C, out, in_, reads, writes, nbias=None, nbias_b=None):
    P = C.P
    if nbias is None:
        ACTF(P, out, in_, AF.Exp, reads=reads, writes=writes, scale=-1.0)
    else:
        ACTF(P, out, in_, AF.Exp, reads=reads + [nbias_b], writes=writes, bias=nbias, scale=-1.0)
    ACTF(P, out, out, AF.Ln, reads=writes + [C.one.b], writes=writes, bias=C.one[:], scale=1.0)
    ACTF(P, out, out, AF.Exp, reads=writes, writes=writes, scale=-1.0)


def emit_gelu(C, dst, src, xs, t1, t2, rd, wr_dst, xsb, t1b, t2b):
    P = C.P
    if xs is not None:
        P.op("act", lambda e: e.copy(out=xs, in_=src), reads=rd, writes=[xsb])
        x, xr = xs, [xsb]
    else:
        x, xr = src, rd
    TT(P, "dve", t1, x, x, ALU.mult, reads=xr, writes=[t1b])
    TS(P, "dve", t1, t1, GELU_C * GELU_S, GELU_S, ALU.mult, ALU.add, reads=[t1b], writes=[t1b])
    TT(P, "dve", t2, t1, x, ALU.mult, reads=[t1b] + xr, writes=[t2b])
    TS(P, "dve", t2, t2, -43.0, None, ALU.max, None, reads=[t2b], writes=[t2b])
    emit_sigmoid(C, t2, t2, [t2b], [t2b])
    TT(P, "dve", dst, t2, x, ALU.mult, reads=[t2b] + xr, writes=wr_dst)


_SIN_C = [-1.0 / 39916800, 1.0 / 362880, -1.0 / 5040, 1.0 / 120, -1.0 / 6, 1.0]
_COS_C = [1.0 / 479001600, -1.0 / 3628800, 1.0 / 40320, -1.0 / 720, 1.0 / 24, -0.5, 1.0]
_MAGIC = 12582912.0


def emit_sincos(P, sin_out, cos_out, ang, kf, r, h, p, S, Cp, bufs):
    TWO_PI = 2.0 * math.pi
    C1 = 6.28125
    C2 = TWO_PI - C1
    B = bufs
    TS(P, "dve", kf, ang, 1.0 / TWO_PI, None, ALU.mult, None, reads=[B["ang"]], writes=[B["kf"]])
    TS(P, "dve", kf, kf, _MAGIC, None, ALU.add, None, reads=[B["kf"]], writes=[B["kf"]])
    TS(P, "dve", kf, kf, -_MAGIC, None, ALU.add, None, reads=[B["kf"]], writes=[B["kf"]])
    STT(P, "dve", r, kf, -C1, ang, ALU.mult, ALU.add, reads=[B["kf"], B["ang"]], writes=[B["r"]])
    STT(P, "dve", r, kf, -C2, r, ALU.mult, ALU.add, reads=[B["kf"], B["r"]], writes=[B["r"]])
    TS(P, "dve", h, r, 0.5, None, ALU.mult, None, reads=[B["r"]], writes=[B["h"]])
    TT(P, "dve", p, h, h, ALU.mult, reads=[B["h"]], writes=[B["p"]])
    TS(P, "dve", S, p, _SIN_C[0], _SIN_C[1], ALU.mult, ALU.add, reads=[B["p"]], writes=[B["S"]])
    for c in _SIN_C[2:]:
        TT(P, "dve", S, S, p, ALU.mult, reads=[B["S"], B["p"]], writes=[B["S"]])
        TS(P, "dve", S, S, c, None, ALU.add, None, reads=[B["S"]], writes=[B["S"]])
    TT(P, "dve", S, S, h, ALU.mult, reads=[B["S"], B["h"]], writes=[B["S"]])
    TS(P, "dve", Cp, p, _COS_C[0], _COS_C[1], ALU.mult, ALU.add, reads=[B["p"]], writes=[B["Cp"]])
    for c in _COS_C[2:]:
        TT(P, "dve", Cp, Cp, p, ALU.mult, reads=[B["Cp"], B["p"]], writes=[B["Cp"]])
        TS(P, "dve", Cp, Cp, c, None, ALU.add, None, reads=[B["Cp"]], writes=[B["Cp"]])
    STT(P, "dve", sin_out, S, 2.0, Cp, ALU.mult, ALU.mult, reads=[B["S"], B["Cp"]], writes=[B["sin"]])
    TT(P, "dve", cos_out, S, S, ALU.mult, reads=[B["S"]], writes=[B["cos"]])
    TS(P, "dve", cos_out, cos_out, -2.0, 1.0, ALU.mult, ALU.add, reads=[B["cos"]], writes=[B["cos"]])


def phase_rope_tables(C, positions, invf, ROT):
    P, S = C.P, C.S
    ph = Phase(C)
    CH = min(S, 2048)
    pos_i = ph.tile("pos_i", [32, CH], I32)
    names = ("ang", "kf", "r", "h", "p", "S", "Cp", "sin", "cos")
    T_ = {n: ph.tile("rp_" + n, [32, CH], F32) for n in names}
    bufs = {n: T_[n].b for n in names}
    iv = ph.tile("iv", [32, 1], F32)
    P.dma("sp", [(iv[:], invf)], writes=[iv.b])
    for c0 in range(0, S, CH):
        P.dma("sp", [(pos_i[:], positions[c0:c0 + CH].rearrange("(o s) -> o s", o=1).broadcast_to([32, CH]))], writes=[pos_i.b])
        CP(P, "dve", T_["ang"][:], pos_i[:], reads=[pos_i.b], writes=[bufs["ang"]])
        TS(P, "dve", T_["ang"][:], T_["ang"][:], iv[:, 0:1], None, ALU.mult, None, reads=[bufs["ang"], iv.b], writes=[bufs["ang"]])
        emit_sincos(P, T_["sin"][:], T_["cos"][:], T_["ang"][:], T_["kf"][:], T_["r"][:], T_["h"][:], T_["p"][:], T_["S"][:], T_["Cp"][:], bufs)
        P.dma("sp", [(ROT[1, :, c0:c0 + CH], T_["sin"][:])], reads=[bufs["sin"]], owner=T_["sin"].b)
        P.dma("sp", [(ROT[0, :, c0:c0 + CH], T_["cos"][:])], reads=[bufs["cos"]], owner=T_["cos"].b)
    ph.close()


def phase_proj(C, x_in, W, l, ROT, YT, QT, KT, VA, UT, tril):
    P, S = C.P, C.S
    G = 512
    NG = S // G
    ph = Phase(C)
    Win = ph.tile("Win", [128, 8, IN_COLS], BF16)
    win_v = W["w_in"][l].rearrange("(c p) f -> p c f", p=128)
    wst = ph.ring("wst", 2, [128, IN_COLS], F32)
    for dc in range(8):
        load_cast(P, Win[:, dc, :], Win.b, win_v[:, dc, :], wst, "pool" if dc % 2 == 0 else "dve")
    Wkr = ph.tile("Wkr", [128, 8, 96], BF16)
    Wkrr = ph.tile("Wkrr", [128, 8, 96], BF16)
    P.op("dve", lambda e: e.memset(Wkr[:], 0.0), writes=[Wkr.b])
    P.op("dve", lambda e: e.memset(Wkrr[:], 0.0), writes=[Wkrr.b])
    CP(P, "dve", Wkr[:, :, 64:96], Win[:, :, O3:O4], reads=[Win.b], writes=[Wkr.b])
    TS(P, "dve", Wkrr[:, :, 64:80], Win[:, :, O3 + 16:O4], -1.0, None, ALU.mult, None, reads=[Win.b], writes=[Wkrr.b])
    CP(P, "dve", Wkrr[:, :, 80:96], Win[:, :, O3:O3 + 16], reads=[Win.b], writes=[Wkrr.b])
    stg = ph.tile("stg", [128, 2, 768], F32)
    gq = ph.tile("gq", [128, 2], F32)
    P.dma("sp", [(stg[:], W["mla_w_uq"][l].rearrange("(k p) f -> p k f", p=128))], writes=[stg.b])
    with C.nc.allow_non_contiguous_dma(reason="tiny gain vectors"):
        P.dma("sp", [(gq[:], W["mla_q_norm_g"][l].rearrange("(k p) -> p k", p=128))], writes=[gq.b])
    Wq = ph.tile("Wq", [128, 2, 768], BF16)
    Wqr = ph.tile("Wqr", [128, 2, 768], BF16)
    P.op("dve", lambda e: e.memset(Wqr[:], 0.0), writes=[Wqr.b])
    for k in range(2):
        TS(P, "dve", stg[:, k, :], stg[:, k, :], SM_SCALE, None, ALU.mult, None, reads=[stg.b], writes=[stg.b])
        TS(P, "dve", Wq[:, k, :], stg[:, k, :], gq[:, k:k + 1], None, ALU.mult, None, reads=[stg.b, gq.b], writes=[Wq.b])
    Wq4 = Wq[:].rearrange("p k (h d) -> p k h d", h=NH)
    Wqr4 = Wqr[:].rearrange("p k (h d) -> p k h d", h=NH)
    for k in range(2):
        TS(P, "dve", Wqr4[:, k, :, 64:80], Wq4[:, k, :, 80:96], -1.0, None, ALU.mult, None, reads=[Wq.b], writes=[Wqr.b])
        CP(P, "dve", Wqr4[:, k, :, 80:96], Wq4[:, k, :, 64:80], reads=[Wq.b], writes=[Wqr.b])
    stg2 = ph.tile("stg2", [128, 1024], F32)
    gkv = ph.tile("gkv", [128, 1], F32)
    P.dma("sp", [(stg2[:], W["mla_w_ukv"][l])], writes=[stg2.b])
    with C.nc.allow_non_contiguous_dma(reason="tiny gain vectors"):
        P.dma("sp", [(gkv[:], W["mla_kv_norm_g"][l].rearrange("(p o) -> p o", o=1))], writes=[gkv.b])
    Wkn = ph.tile("Wkn", [128, 512], BF16)
    Wv = ph.tile("Wv", [128, 512], BF16)
    stg2v = stg2[:].rearrange("p (h d) -> p h d", h=NH)
    TS(P, "dve", Wkn[:].rearrange("p (h d) -> p h d", h=NH), stg2v[:, :, 0:64], gkv[:, 0:1], None, ALU.mult, None, reads=[stg2.b, gkv.b], writes=[Wkn.b])
    TS(P, "dve", Wv[:].rearrange("p (h d) -> p h d", h=NH), stg2v[:, :, 64:128], gkv[:, 0:1], None, ALU.mult, None, reads=[stg2.b, gkv.b], writes=[Wv.b])
    wsf = ph.tile("wsf", [128, 4, 128], F32)
    trl = ph.tile("trl", [128, 128], F32)
    WsT = ph.tile("WsT", [128, 4, 128], BF16)
    bs = ph.tile("bs", [128, 4], F32)
    Gv = ph.tile("Gv", [128, 256], F32)
    P.dma("sp", [(wsf[:], W["gmlp_ws"][l].rearrange("h t c -> t h c"))], writes=[wsf.b])
    P.dma("sp", [(trl[:], tril)], writes=[trl.b])
    with C.nc.allow_non_contiguous_dma(reason="tiny bias vectors"):
        P.dma("sp", [(bs[:], W["gmlp_bs"][l].rearrange("h t -> t h"))], writes=[bs.b])
    P.dma("sp", [(Gv[:], W["gmlp_norm_g"][l].rearrange("(o d) -> o d", o=1).broadcast_to([128, 256]))], writes=[Gv.b])
    for h in range(4):
        TT(P, "dve", wsf[:, h, :], wsf[:, h, :], trl[:], ALU.mult, reads=[wsf.b, trl.b], writes=[wsf.b])
    pst = C.ps[0]
    for h in range(4):
        P.op("pe", lambda e: e.transpose(out=pst[:, h * 128:(h + 1) * 128], in_=wsf[:, h, :], identity=C.ident[:]),
             reads=[wsf.b, C.ident.b], writes=[pst.b])
    CP(P, "dve", WsT[:].rearrange("p h t -> p (h t)"), pst[:, :], reads=[pst.b], writes=[WsT.b])
    on256 = ph.tile("on256", [128, 128], BF16)
    on128 = ph.tile("on128", [128, 128], BF16)
    P.op("dve", lambda e: e.memset(on256[:], 1.0 / 256), writes=[on256.b])
    P.op("dve", lambda e: e.memset(on128[:], 1.0 / 128), writes=[on128.b])

    XR = ph.ring("XR", 3, [128, D], F32)
    xT = ph.tile("xT", [128, 8, G], BF16)
    xT_b = [Buf("xT%d" % i) for i in range(4)]
    ROTc = ph.ring("ROTc", 2, [128, G], F32)
    ROTs = ph.ring("ROTs", 2, [128, G], F32)
    t1 = ph.ring("t1", 2, [128, 512], F32)
    xsr = ph.ring("xsr", 2, [128, 512], F32)
    t2 = ph.ring("t2", 2, [128, 512], F32)
    guv = ph.ring("guv", 2, [128, 512], F32)
    sq = ph.ring("sq", 2, [128, 256], F32)
    s1 = ph.ring("s1", 2, [128, 4], F32)
    s2 = ph.ring("s2", 2, [128, 4], F32)
    mu = ph.ring("mu", 2, [128, 4], F32)
    var = ph.ring("var", 2, [128, 4], F32)
    vc = ph.ring("vc", 2, [128, 256], F32)
    vn = ph.ring("vn", 2, [128, 256], BF16)
    ya = ph.ring("ya", 2, [128, 256], F32)
    ssq = ph.ring("ssq", 2, [128, 1], F32)
    bst = ph.ring("bst", 6, [128, 6], F32)
    bmv = ph.ring("bmv", 6, [128, 2], F32)
    yaT = ph.ring("yaT", 2, [128, 2, G], BF16)
    usb = ph.ring("usb", 2, [128, 2, G], F32)
    csb = ph.ring("csb", 2, [128, 3, G], F32)
    csq = ph.ring("csq", 2, [128, 3, G], BF16)
    rbc = ph.ring("rbc", 2, [128, 2, G], F32)
    cn = ph.ring("cn", 2, [128, 3, G], BF16)
    knb = ph.ring("knb", 2, [128, G], BF16)
    vsb = ph.ring("vsb", 2, [128, 512], BF16)
    krb = ph.ring("krb", 2, [128, G], BF16)
    rtmp = ph.ring("rtmp", 3, [128, G], F32)
    qtb = ph.ring("qtb", 3, [128, G], BF16)
    ps_t = PsRing(C.ps[0:2])
    ps_g = PsRing(C.ps[2:8])

    for g in range(NG):
        c0 = g * G
        cs = slice(c0, c0 + G)
        rc = ROTc.next()
        rs = ROTs.next()
        P.dma("sp", [(rc[64:96, :], ROT[0, :, cs])], writes=[rc.b])
        P.dma("sp", [(rs[64:96, :], ROT[1, :, cs])], writes=[rs.b])
        for s in range(4):
            X = XR.next()
            r0 = c0 + s * 128
            P.dma("sp", [(X[:], x_in[r0:r0 + 128, :])], writes=[X.b])
            emit_transpose_x(C, X, xT, xT_b[s], s, ps_t)

        c_t = csb.next()
        u_t = usb.next()
        q_t = csq.next()
        for i, col in enumerate((O1, O1 + 128, O2, O4, O4 + 128)):
            pp = ps_g.next()
            for dc in range(8):
                MM(P, pp[:, :], Win[:, dc, col:col + 128], xT[:, dc, :], dc == 0, dc == 7, reads=[Win.b] + xT_b, writes=[pp.b])
            if i < 3:
                CP(P, "act", c_t[:, i, :], pp[:, :], reads=[pp.b], writes=[c_t.b])
                TT(P, "dve", q_t[:, i, :], pp[:, :], c_t[:, i, :], ALU.mult, reads=[pp.b, c_t.b], writes=[q_t.b])
            else:
                CP(P, "act", u_t[:, i - 3, :], pp[:, :], reads=[pp.b], writes=[u_t.b])
        P.dma("sp", [(UT[:, cs].rearrange("(k p) t -> p k t", p=128), u_t[:])], reads=[u_t.b])
        r_t = rbc.next()
        pq = ps_g.next()
        MM(P, pq[:, :], on256[:], q_t[:, 0, :], True, False, reads=[on256.b, q_t.b], writes=[pq.b])
        MM(P, pq[:, :], on256[:], q_t[:, 1, :], False, True, reads=[on256.b, q_t.b], writes=[pq.b])
        emit_rsqrt(C, r_t[:, 0, :], pq[:, :], 1.0, C.eps_rms, [pq.b], [r_t.b])
        pk = ps_g.next()
        MM(P, pk[:, :], on128[:], q_t[:, 2, :], True, True, reads=[on128.b, q_t.b], writes=[pk.b])
        emit_rsqrt(C, r_t[:, 1, :], pk[:, :], 1.0, C.eps_rms, [pk.b], [r_t.b])
        n_t = cn.next()
        for i in range(3):
            TT(P, "dve", n_t[:, i, :], c_t[:, i, :], r_t[:, 0 if i < 2 else 1, :], ALU.mult, reads=[c_t.b, r_t.b], writes=[n_t.b])
        for hp in range(4):
            pp = ps_g.next()
            MM(P, pp[:, :], Wkn[:, hp * 128:(hp + 1) * 128], n_t[:, 2, :], True, True, reads=[Wkn.b, n_t.b], writes=[pp.b])
            kb_ = knb.next()
            CP(P, "act", kb_[:], pp[:, :], reads=[pp.b], writes=[kb_.b])
            P.dma("sp", [(KT[2 * hp, 0:64, cs], kb_[0:64, :]), (KT[2 * hp + 1, 0:64, cs], kb_[64:128, :])], reads=[kb_.b])
        for s in range(4):
            pp = ps_g.next()
            MM(P, pp[:, :], n_t[:, 2, s * 128:(s + 1) * 128], Wv[:], True, True, reads=[Wv.b, n_t.b], writes=[pp.b])
            v_ = vsb.next()
            CP(P, "act", v_[:], pp[:, :], reads=[pp.b], writes=[v_.b])
            P.dma("sp", [(VA[c0 + s * 128:c0 + (s + 1) * 128, :], v_[:])], reads=[v_.b])
        pa = ps_g.next()
        pb = ps_g.next()
        for dc in range(8):
            MM(P, pa[0:96, :], Wkr[:, dc, :], xT[:, dc, :], dc == 0, dc == 7, reads=[Wkr.b] + xT_b, writes=[pa.b])
        for dc in range(8):
            MM(P, pb[0:96, :], Wkrr[:, dc, :], xT[:, dc, :], dc == 0, dc == 7, reads=[Wkrr.b] + xT_b, writes=[pb.b])
        ra = rtmp.next()
        rb = rtmp.next()
        kr_ = krb.next()
        TT(P, "dve", ra[64:96, :], pa[64:96, :], rc[64:96, :], ALU.mult, reads=[pa.b, rc.b], writes=[ra.b])
        TT(P, "dve", rb[64:96, :], pb[64:96, :], rs[64:96, :], ALU.mult, reads=[pb.b, rs.b], writes=[rb.b])
        TT(P, "dve", kr_[64:96, :], ra[64:96, :], rb[64:96, :], ALU.add, reads=[ra.b, rb.b], writes=[kr_.b])
        P.dma("sp", [(KT[h, 64:96, cs], kr_[64:96, :]) for h in range(NH)], reads=[kr_.b])
        for h in range(NH):
            pa = ps_g.next()
            pb = ps_g.next()
            for k in range(2):
                MM(P, pa[0:96, :], Wq[:, k, h * 96:(h + 1) * 96], n_t[:, k, :], k == 0, k == 1, reads=[Wq.b, n_t.b], writes=[pa.b])
            for k in range(2):
                MM(P, pb[0:96, :], Wqr[:, k, h * 96:(h + 1) * 96], n_t[:, k, :], k == 0, k == 1, reads=[Wqr.b, n_t.b], writes=[pb.b])
            qt_ = qtb.next()
            ra = rtmp.next()
            rb = rtmp.next()
            CP(P, "act", qt_[0:64, :], pa[0:64, :], reads=[pa.b], writes=[qt_.b])
            TT(P, "dve", ra[64:96, :], pa[64:96, :], rc[64:96, :], ALU.mult, reads=[pa.b, rc.b, qt_.b], writes=[ra.b])
            TT(P, "dve", rb[64:96, :], pb[64:96, :], rs[64:96, :], ALU.mult, reads=[pb.b, rs.b], writes=[rb.b])
            TT(P, "dve", qt_[64:96, :], ra[64:96, :], rb[64:96, :], ALU.add, reads=[ra.b, rb.b], writes=[qt_.b])
            P.dma("sp", [(QT[h, :, cs], qt_[0:96, :])], reads=[qt_.b])

        yT = yaT.next()
        for s in range(4):
            pp = ps_g.next()
            for dc in range(8):
                MM(P, pp[:, :], xT[:, dc, s * 128:(s + 1) * 128], Win[:, dc, 0:512], dc == 0, dc == 7, reads=[Win.b, xT_b[s]], writes=[pp.b])
            a1 = t1.next()
            a2 = t2.next()
            gg = guv.next()
            xs_ = xsr.next()
            emit_gelu(C, gg[:], pp[:, :], xs_[:], a1[:], a2[:], [pp.b], [gg.b], xs_.b, a1.b, a2.b)
            vc_, vn_ = vc.next(), vn.next()
            for h in range(4):
                st_, mv_ = bst.next(), bmv.next()
                P.op("dve", lambda e: e.bn_stats(out=st_[:], in_=gg[:, 256 + h * 64:256 + (h + 1) * 64]), reads=[gg.b], writes=[st_.b])
                P.op("dve", lambda e: e.bn_aggr(out=mv_[:], in_=st_[:]), reads=[st_.b], writes=[mv_.b])
                emit_rsqrt(C, mv_[:, 1:2], mv_[:, 1:2], 1.0, C.eps_ln, [mv_.b], [mv_.b])
                TS(P, "dve", vc_[:, h * 64:(h + 1) * 64], gg[:, 256 + h * 64:256 + (h + 1) * 64], mv_[:, 0:1], mv_[:, 1:2], ALU.subtract, ALU.mult,
                   reads=[gg.b, mv_.b], writes=[vc_.b])
            TT(P, "dve", vn_[:], vc_[:], Gv[:], ALU.mult, reads=[vc_.b, Gv.b], writes=[vn_.b])
            pz = ps_g.next()
            for h in range(4):
                MM(P, pz[:, h * 64:(h + 1) * 64], WsT[:, h, :], vn_[:, h * 64:(h + 1) * 64], True, True, reads=[WsT.b, vn_.b], writes=[pz.b])
            ya_ = ya.next()
            for h in range(4):
                STT(P, "dve", ya_[:, h * 64:(h + 1) * 64], pz[:, h * 64:(h + 1) * 64], bs[:, h:h + 1], gg[:, h * 64:(h + 1) * 64], ALU.add, ALU.mult,
                    reads=[pz.b, bs.b, gg.b], writes=[ya_.b])
            st_, mv_, ss_ = bst.next(), bmv.next(), ssq.next()
            P.op("dve", lambda e: e.bn_stats(out=st_[:], in_=ya_[:]), reads=[ya_.b], writes=[st_.b])
            P.op("dve", lambda e: e.bn_aggr(out=mv_[:], in_=st_[:]), reads=[st_.b], writes=[mv_.b])
            STT(P, "dve", ss_[:], mv_[:, 0:1], mv_[:, 0:1], mv_[:, 1:2], ALU.mult, ALU.add, reads=[mv_.b], writes=[ss_.b])
            emit_rsqrt(C, ss_[:], ss_[:], 1.0, C.eps_rms, [ss_.b], [ss_.b])
            TS(P, "dve", ya_[:], ya_[:], ss_[:, 0:1], None, ALU.mult, None, reads=[ya_.b, ss_.b], writes=[ya_.b])
            py = ps_t.next()
            for k in range(2):
                P.op("pe", lambda e: e.transpose(out=py[:, k * 128:(k + 1) * 128], in_=ya_[:, k * 128:(k + 1) * 128], identity=C.ident[:]),
                     reads=[ya_.b, C.ident.b], writes=[py.b])
            CP(P, "act", yT[:, :, s * 128:(s + 1) * 128], py[:, 0:256].rearrange("p (k t) -> p k t", k=2), reads=[py.b], writes=[yT.b])
        P.dma("sp", [(YT[0:256, cs].rearrange("(k p) t -> p k t", p=128), yT[:])], reads=[yT.b])
    ph.close()


def gen_attn(C, ph, QT, KT, VA, YT, triu, shiftm, bank_s, bank_o, bank_r, kv_ring):
    P, S = C.P, C.S
    NQG = S // 512
    NKB = S // 128
    Kh = ph.ring("Kh", kv_ring, [128, S], BF16)
    Vh = ph.ring("Vh", kv_ring, [128, NKB, 128], BF16)
    for v in Vh.tiles:
        P.op("dve", lambda e: e.memset(v[:, :, 64:128], 1.0), writes=[v.b])
    tri = ph.tile("tri", [128, 128], BF16)
    shf32 = ph.tile("shf32", [128, 128], F32)
    shf = ph.tile("shf", [128, 128], BF16)
    trf = ph.tile("trf", [128, 128], F32)
    P.dma("sp", [(trf[:], triu)], writes=[trf.b])
    CP(P, "dve", tri[:], trf[:], reads=[trf.b], writes=[tri.b])
    P.dma("sp", [(shf32[:], shiftm)], writes=[shf32.b])
    CP(P, "dve", shf[:], shf32[:], reads=[shf32.b], writes=[shf.b])
    qT = ph.ring("qT", 3, [128, 512], BF16)
    PT = ph.ring("PT", 4, [128, 512], BF16)
    ER = ph.ring("ER", 2, [128, 512], F32)
    EH = ph.ring("EH", 2, [128, 512], BF16)
    EL = ph.ring("EL", 2, [128, 512], BF16)
    for t_ in EH.tiles + EL.tiles:
        P.op("dve", lambda e: e.memset(t_[:], 0.0), writes=[t_.b])
    YB = ph.ring("YB", 2, [128, 512], BF16)
    ps_s = PsRing(bank_s)
    ps_o = PsRing(bank_o)
    ps_r = PsRing(bank_r)

    blocks = [(h, j, kb) for h in range(NH) for j in range(NQG) for kb in range(4 * (j + 1))]
    cur = {}
    hstate = {}
    jstate = {}

    def emit_S(i):
        h, j, kb = blocks[i]
        if h not in hstate:
            K_, V_ = Kh.next(), Vh.next()
            P.dma("sp", [(K_[0:96, :], KT[h])], writes=[K_.b])
            vsrc = VA[:, h * 64:(h + 1) * 64].rearrange("(kb p) d -> p kb d", p=128)
            step = max(1, min(8, NKB))
            P.dma("sp", [(V_[:, a:a + step, 0:64], vsrc[:, a:a + step, :]) for a in range(0, NKB, step)], writes=[V_.b])
            hstate.clear()
            hstate[h] = (K_, V_)
        K_, V_ = hstate[h]
        if (h, j) not in jstate:
            q_ = qT.next()
            P.dma("sp", [(q_[0:96, :], QT[h, :, j * 512:(j + 1) * 512])], writes=[q_.b])
            jstate.clear()
            jstate[(h, j)] = (q_, ps_o.next())
        q_, O_ = jstate[(h, j)]
        m = kb - 4 * j
        lo = 128 * m if m > 0 else 0
        ps = ps_s.next()
        MM(P, ps[:, lo:512], K_[0:96, kb * 128:(kb + 1) * 128], q_[0:96, lo:512], True, True, reads=[K_.b, q_.b], writes=[ps.b])
        pt = PT.next()
        ACTF(P, pt[:, lo:512], ps[:, lo:512], AF.Exp, reads=[ps.b], writes=[pt.b])
        if m >= 0:
            TT(P, "dve", pt[:, lo:lo + 128], pt[:, lo:lo + 128], tri[:], ALU.mult, reads=[pt.b, tri.b], writes=[pt.b])
        cur[i] = (pt, lo, V_, O_)

    def emit_PV(i):
        h, j, kb = blocks[i]
        pt, lo, V_, O_ = cur.pop(i)
        last = 4 * (j + 1) - 1
        MM(P, O_[:, lo:512], V_[:, kb, :], pt[:, lo:512], kb == 0, kb == last, reads=[V_.b, pt.b], writes=[O_.b])
        if kb == last:
            E = ER.next()
            CP(P, "act", E[0:64, :], O_[0:64, :], reads=[O_.b], writes=[E.b])
            P.op("dve", lambda e: e.reciprocal(out=E[64:128, :], in_=O_[64:128, :]), reads=[O_.b], writes=[E.b])
            eh, el = EH.next(), EL.next()
            CP(P, "dve", eh[64:128, :], E[64:128, :], reads=[E.b], writes=[eh.b])
            TT(P, "dve", el[64:128, :], E[64:128, :], eh[64:128, :], ALU.subtract, reads=[E.b, eh.b], writes=[el.b])
            pr = ps_r.next()
            MM(P, pr[:, :], shf[:], eh[:], True, False, reads=[shf.b, eh.b], writes=[pr.b])
            MM(P, pr[:, :], shf[:], el[:], False, True, reads=[shf.b, el.b], writes=[pr.b])
            yb = YB.next()
            TT(P, "dve", yb[0:64, :], E[0:64, :], pr[0:64, :], ALU.mult, reads=[E.b, pr.b], writes=[yb.b])
            P.dma("sp", [(YT[256 + h * 64:256 + (h + 1) * 64, j * 512:(j + 1) * 512], yb[0:64, :])], reads=[yb.b])

    LOOK = 2
    n = len(blocks)
    pend = []
    for i in range(n):
        if i > 0 and blocks[i][0] != blocks[i - 1][0]:
            while pend:
                emit_PV(pend.pop(0))
        emit_S(i)
        pend.append(i)
        if len(pend) > LOOK:
            emit_PV(pend.pop(0))
        yield
    while pend:
        emit_PV(pend.pop(0))
    yield


def phase_attn(C, QT, KT, VA, YT, triu, shiftm):
    ph = Phase(C)
    for _ in gen_attn(C, ph, QT, KT, VA, YT, triu, shiftm, C.ps[0:3], C.ps[3:6], C.ps[6:8], 2):
        pass
    ph.close()


def gen_s5(C, ph, UT, YT, W, l, maskC, bank_b, bank_y, tbanks, nb):
    P, S, nc = C.P, C.S, C.nc
    T = 512
    NCH = S // T
    TWO_PI = 2.0 * math.pi
    f8 = lambda nm: ph.tile(nm, [128, 8], F32)
    ar, ai, ldt, dt, mag, th, kf, rr, mm_, sn, cs_, abr, abi, den, am1, cr, ci, tq = [f8(n) for n in
        ("ar", "ai", "ldt", "dt", "mag", "th", "kf", "rr", "mm_", "sn", "cs_", "abr", "abi", "den", "am1", "cr", "ci", "tq")]
    ki = ph.tile("ki", [128, 8], I32)
    with nc.allow_non_contiguous_dma(reason="tiny ssm parameter vectors"):
        a_re_v = W["ssm_a_re"][l].rearrange("(j two) p -> two p j", two=2)
        a_im_v = W["ssm_a_im"][l].rearrange("(j two) p -> two p j", two=2)
        P.dma("sp", [(ar[0:64, :], a_re_v[0]), (ar[64:128, :], a_re_v[1])], writes=[ar.b])
        P.dma("sp", [(ai[0:64, :], a_im_v[0]), (ai[64:128, :], a_im_v[1])], writes=[ai.b])
        ldt_v = W["ssm_log_dt"][l].rearrange("(j two) -> two j", two=2)
        P.dma("sp", [(ldt[0:64, :], ldt_v[0:1, :].broadcast_to([64, 8])), (ldt[64:128, :], ldt_v[1:2, :].broadcast_to([64, 8]))], writes=[ldt.b])
    ACTF(P, dt[:], ldt[:], AF.Exp, reads=[ldt.b], writes=[dt.b])
    TT(P, "dve", tq[:], ar[:], dt[:], ALU.mult, reads=[ar.b, dt.b], writes=[tq.b])
    ACTF(P, mag[:], tq[:], AF.Exp, reads=[tq.b], writes=[mag.b])
    TT(P, "dve", th[:], ai[:], dt[:], ALU.mult, reads=[ai.b, dt.b], writes=[th.b])

    sc_names = ("ang", "kf", "r", "h", "p", "S", "Cp", "sin", "cos")
    sc_t = {"ang": th, "kf": kf, "r": rr, "h": mm_, "p": f8("scp"), "S": f8("scS"), "Cp": f8("scC"), "sin": sn, "cos": cs_}
    emit_sincos(P, sn[:], cs_[:], th[:], kf[:], rr[:], mm_[:], sc_t["p"][:], sc_t["S"][:], sc_t["Cp"][:], {n: sc_t[n].b for n in sc_names})
    TT(P, "dve", abr[:], mag[:], cs_[:], ALU.mult, reads=[mag.b, cs_.b], writes=[abr.b])
    TT(P, "dve", abi[:], mag[:], sn[:], ALU.mult, reads=[mag.b, sn.b], writes=[abi.b])
    TT(P, "dve", den[:], ar[:], ar[:], ALU.mult, reads=[ar.b], writes=[den.b])
    TT(P, "dve", tq[:], ai[:], ai[:], ALU.mult, reads=[ai.b], writes=[tq.b])
    TT(P, "dve", den[:], den[:], tq[:], ALU.add, reads=[den.b, tq.b], writes=[den.b])
    P.op("dve", lambda e: e.reciprocal(out=den[:], in_=den[:]), reads=[den.b], writes=[den.b])
    TS(P, "dve", am1[:], abr[:], -1.0, None, ALU.add, None, reads=[abr.b], writes=[am1.b])
    TT(P, "dve", cr[:], am1[:], ar[:], ALU.mult, reads=[am1.b, ar.b], writes=[cr.b])
    TT(P, "dve", tq[:], abi[:], ai[:], ALU.mult, reads=[abi.b, ai.b], writes=[tq.b])
    TT(P, "dve", cr[:], cr[:], tq[:], ALU.add, reads=[cr.b, tq.b], writes=[cr.b])
    TT(P, "dve", cr[:], cr[:], den[:], ALU.mult, reads=[cr.b, den.b], writes=[cr.b])
    TT(P, "dve", ci[:], abi[:], ar[:], ALU.mult, reads=[abi.b, ar.b], writes=[ci.b])
    TT(P, "dve", tq[:], am1[:], ai[:], ALU.mult, reads=[am1.b, ai.b], writes=[tq.b])
    TT(P, "dve", ci[:], ci[:], tq[:], ALU.subtract, reads=[ci.b, tq.b], writes=[ci.b])
    TT(P, "dve", ci[:], ci[:], den[:], ALU.mult, reads=[ci.b, den.b], writes=[ci.b])

    br = ph.tile("br", [128, 8, 16], F32)
    bi = ph.tile("bi", [128, 8, 16], F32)
    bbr = ph.tile("bbr", [128, 8, 16], F32)
    bbi = ph.tile("bbi", [128, 8, 16], F32)
    tb = ph.tile("tb", [128, 8, 16], F32)
    b_re_v = W["ssm_b_re"][l].rearrange("(j two) p c -> two p j c", two=2)
    b_im_v = W["ssm_b_im"][l].rearrange("(j two) p c -> two p j c", two=2)
    P.dma("sp", [(br[0:64], b_re_v[0]), (br[64:128], b_re_v[1])], writes=[br.b])
    P.dma("sp", [(bi[0:64], b_im_v[0]), (bi[64:128], b_im_v[1])], writes=[bi.b])
    for j in range(8):
        TS(P, "dve", tb[:, j, :], bi[:, j, :], ci[:, j:j + 1], None, ALU.mult, None, reads=[bi.b, ci.b], writes=[tb.b])
        STT(P, "dve", bbr[:, j, :], br[:, j, :], cr[:, j:j + 1], tb[:, j, :], ALU.mult, ALU.subtract, reads=[br.b, cr.b, tb.b], writes=[bbr.b])
        TS(P, "dve", tb[:, j, :], br[:, j, :], ci[:, j:j + 1], None, ALU.mult, None, reads=[br.b, ci.b, bbr.b], writes=[tb.b])
        STT(P, "dve", bbi[:, j, :], bi[:, j, :], cr[:, j:j + 1], tb[:, j, :], ALU.mult, ALU.add, reads=[bi.b, cr.b, tb.b], writes=[bbi.b])
    Mz = ph.tile("Mz", [128, 16, 128], F32)
    BT = ph.tile("BT", [128, 16, 128], BF16)
    P.op("dve", lambda e: e.memset(Mz[:], 0.0), writes=[Mz.b])
    for j in range(8):
        ch0 = (j % 4) * 32
        for ri, bb in enumerate((bbr, bbi)):
            CP(P, "dve", Mz[0:64, 2 * j + ri, ch0:ch0 + 16], bb[0:64, j, :], reads=[bb.b], writes=[Mz.b])
            CP(P, "dve", Mz[64:128, 2 * j + ri, ch0 + 16:ch0 + 32], bb[64:128, j, :], reads=[bb.b], writes=[Mz.b])
    for q in range(4):
        pst = tbanks[q % 2]
        for k in range(4):
            P.op("pe", lambda e: e.transpose(out=pst[:, k * 128:(k + 1) * 128], in_=Mz[:, 4 * q + k, :], identity=C.ident[:]),
                 reads=[Mz.b, C.ident.b], writes=[pst.b])
        CP(P, "dve", BT[:, 4 * q:4 * q + 4, :].rearrange("p a n -> p (a n)"), pst[:, :], reads=[pst.b], writes=[BT.b])
    Cn2 = ph.tile("Cn2", [128, 4, 128], F32)
    CT = ph.tile("CT", [128, 16, 128], BF16)
    mC = ph.tile("mC", [128, 4, 128], F32)
    P.dma("sp", [(mC[:], maskC)], writes=[mC.b])
    for ri, nm in enumerate(("ssm_c_re", "ssm_c_im")):
        cv = W[nm][l].rearrange("(cc gl) c p -> cc (gl c) p", cc=2)
        for cc in range(2):
            P.dma("sp", [(Cn2[:, cc * 2 + ri, 0:64], cv[cc]), (Cn2[:, cc * 2 + ri, 64:128], cv[cc])], writes=[Cn2.b])
    pst = tbanks[2]
    for k in range(4):
        P.op("pe", lambda e: e.transpose(out=pst[:, k * 128:(k + 1) * 128], in_=Cn2[:, k, :], identity=C.ident[:]),
             reads=[Cn2.b, C.ident.b], writes=[pst.b])
    for cc in range(2):
        for j4 in range(4):
            j = cc * 4 + j4
            TT(P, "dve", CT[:, 2 * j, :], pst[:, (cc * 2) * 128:(cc * 2 + 1) * 128], mC[:, j4, :], ALU.mult, reads=[pst.b, mC.b], writes=[CT.b])
            STT(P, "dve", CT[:, 2 * j + 1, :], pst[:, (cc * 2 + 1) * 128:(cc * 2 + 2) * 128], -1.0, mC[:, j4, :], ALU.mult, ALU.mult,
                reads=[pst.b, mC.b], writes=[CT.b])
    dsk = ph.tile("dsk", [128, 2], F32)
    bglu = ph.tile("bglu", [128, 2], F32)
    Wglu = ph.tile("Wglu", [128, 2, 256], BF16)
    on256 = ph.tile("on256", [128, 128], BF16)
    P.op("dve", lambda e: e.memset(on256[:], 1.0 / 256), writes=[on256.b])
    with nc.allow_non_contiguous_dma(reason="tiny ssm parameter vectors"):
        P.dma("sp", [(dsk[:], W["ssm_d"][l].rearrange("g c -> (g c)").rearrange("(cc p) -> p cc", p=128))], writes=[dsk.b])
        P.dma("sp", [(bglu[:], W["ssm_glu_b"][l].rearrange("(cc p) -> p cc", p=128))], writes=[bglu.b])
    nbglu = ph.tile("nbglu", [128, 2], F32)
    TS(P, "dve", nbglu[:], bglu[:], -1.0, None, ALU.mult, None, reads=[bglu.b], writes=[nbglu.b])
    wgs = ph.tile("wgs", [128, 2, 256], F32)
    P.dma("sp", [(wgs[:], W["ssm_glu_w"][l].rearrange("(k p) f -> p k f", p=128))], writes=[wgs.b])
    CP(P, "dve", Wglu[:], wgs[:], reads=[wgs.b], writes=[Wglu.b])
    COS = ph.tile("COS", [128, 8, T], F32)
    SIN = ph.tile("SIN", [128, 8, T], F32)
    tmp = ph.tile("tmpd", [128, 8, T // 2], F32)
    CP(P, "dve", COS[:, :, 0], cs_[:], reads=[cs_.b], writes=[COS.b])
    CP(P, "dve", SIN[:, :, 0], sn[:], reads=[sn.b], writes=[SIN.b])
    w = 1
    while w < T:
        for j in range(8):
            cj = COS[:, j, w - 1:w]
            sj = SIN[:, j, w - 1:w]
            TS(P, "dve", tmp[:, j, 0:w], SIN[:, j, 0:w], sj, None, ALU.mult, None, reads=[SIN.b], writes=[tmp.b])
            STT(P, "dve", COS[:, j, w:2 * w], COS[:, j, 0:w], cj, tmp[:, j, 0:w], ALU.mult, ALU.subtract, reads=[COS.b, tmp.b], writes=[COS.b])
            TS(P, "dve", tmp[:, j, 0:w], SIN[:, j, 0:w], cj, None, ALU.mult, None, reads=[SIN.b, COS.b], writes=[tmp.b])
            STT(P, "dve", SIN[:, j, w:2 * w], COS[:, j, 0:w], sj, tmp[:, j, 0:w], ALU.mult, ALU.add, reads=[COS.b, SIN.b, tmp.b], writes=[SIN.b])
        w *= 2
    if USE_HW_SCAN:
        DEC = ph.tile("DEC", [128, 8, T], F32)
        P.op("dve", lambda e: e.memset(DEC[:], 1.0), writes=[DEC.b])
        for j in range(8):
            TS(P, "dve", DEC[:, j, :], DEC[:, j, :], mag[:, j:j + 1], None, ALU.mult, None, reads=[DEC.b, mag.b], writes=[DEC.b])
    NST = int(math.log2(T))
    RP = ph.tile("RP", [128, NST, 8], F32)
    CP(P, "dve", RP[:, 0, :], mag[:], reads=[mag.b], writes=[RP.b])
    for k in range(1, NST):
        TT(P, "dve", RP[:, k, :], RP[:, k - 1, :], RP[:, k - 1, :], ALU.mult, reads=[RP.b], writes=[RP.b])
    car_re = ph.tile("car_re", [128, 8], F32)
    car_im = ph.tile("car_im", [128, 8], F32)
    car_b = [Buf("car%d" % j) for j in range(8)]
    P.op("dve", lambda e: e.memset(car_re[:], 0.0), writes=[car_re.b])
    P.op("dve", lambda e: e.memset(car_im[:], 0.0), writes=[car_im.b])
    for j in range(8):
        car_b[j].w = car_im.b.w

    uT = ph.ring("uT", nb, [128, 2, T], F32)
    ub = ph.ring("ub", nb, [128, 2, T], BF16)
    A_ = ph.ring("A_", nb, [128, T], F32)
    B_ = ph.ring("B_", nb, [128, T], F32)
    C_ = ph.ring("C_", nb, [128, T], F32)
    D_ = ph.ring("D_", nb, [128, T], F32)
    gre = ph.ring("gre", 2 if USE_HW_SCAN else 0, [128, T], F32)
    gim = ph.ring("gim", 2 if USE_HW_SCAN else 0, [128, T], F32)
    ctmp = ph.ring("ctmp", 8, [128, 1], F32)
    sc0 = ph.ring("sc0", nb, [128, 2, T], F32)
    sc1 = ph.ring("sc1", nb, [128, 2, T], F32)
    H = ph.ring("H", 1, [128, 16, T], BF16)
    ysb = ph.ring("ysb", 1, [128, 2, T], F32)
    g1 = ph.ring("g1", 1, [128, 2, T], F32)
    g2 = ph.ring("g2", 1, [128, 2, T], F32)
    yg = ph.ring("yg", 1, [128, 2, T], F32)
    ygb = ph.ring("ygb", nb, [128, 2, T], BF16)
    sig = ph.ring("sig", nb, [128, T], F32)
    yc = ph.ring("yc", 1, [128, 2, T], F32)
    ysq = ph.ring("ysq", nb, [128, 2, T], BF16)
    rr_ = ph.ring("rr_", nb, [128, T], F32)
    ycn = ph.ring("ycn", nb, [128, 2, T], BF16)
    ps_b = PsRing(bank_b)
    ps_y = PsRing(bank_y)

    for c in range(NCH):
        cs = slice(c * T, (c + 1) * T)
        u_ = uT.next()
        ub_ = ub.next()
        P.dma("sp", [(u_[:], UT[:, cs].rearrange("(k p) t -> p k t", p=128))], writes=[u_.b])
        CP(P, "act", ub_[:], u_[:], reads=[u_.b], writes=[ub_.b])
        H_ = H.next()
        for j in range(8):
            cc = j // 4
            p_re = ps_b.next()
            p_im = ps_b.next()
            MM(P, p_re[:, :], BT[:, 2 * j, :], ub_[:, cc, :], True, True, reads=[BT.b, ub_.b], writes=[p_re.b])
            MM(P, p_im[:, :], BT[:, 2 * j + 1, :], ub_[:, cc, :], True, True, reads=[BT.b, ub_.b], writes=[p_im.b])
            a, b, c2, d = A_.next(), B_.next(), C_.next(), D_.next()
            TT(P, "dve", a[:], p_re[:, :], COS[:, j, :], ALU.mult, reads=[p_re.b, COS.b], writes=[a.b])
            TT(P, "dve", b[:], p_im[:, :], SIN[:, j, :], ALU.mult, reads=[p_im.b, SIN.b], writes=[b.b])
            TT(P, "dve", c2[:], p_im[:, :], COS[:, j, :], ALU.mult, reads=[p_im.b, COS.b], writes=[c2.b])
            TT(P, "dve", d[:], p_re[:, :], SIN[:, j, :], ALU.mult, reads=[p_re.b, SIN.b], writes=[d.b])
            TT(P, "dve", a[:], a[:], b[:], ALU.add, reads=[a.b, b.b], writes=[a.b])
            TT(P, "dve", c2[:], c2[:], d[:], ALU.subtract, reads=[c2.b, d.b], writes=[c2.b])
            if USE_HW_SCAN:
                gr_t, gi_t = gre.next(), gim.next()
                gr, gi = gr_t[:], gi_t[:]
                gr_b, gi_b = gr_t.b, gi_t.b
                dec = DEC[:, j, :]
                P.op("dve", lambda e: e.tensor_tensor_scan(out=gr, data0=dec, data1=a[:], initial=car_re[:, j:j + 1], op0=ALU.mult, op1=ALU.add),
                     reads=[DEC.b, a.b, car_b[j]], writes=[gr_b])
                P.op("dve", lambda e: e.tensor_tensor_scan(out=gi, data0=dec, data1=c2[:], initial=car_im[:, j:j + 1], op0=ALU.mult, op1=ALU.add),
                     reads=[DEC.b, c2.b, car_b[j]], writes=[gi_b])
            else:
                s0, s1_ = sc0.next(), sc1.next()
                STT(P, "dve", s0[:, 0, 0:1], car_re[:, j:j + 1], mag[:, j:j + 1], a[:, 0:1], ALU.mult, ALU.add, reads=[car_b[j], mag.b, a.b], writes=[s0.b])
                STT(P, "dve", s0[:, 1, 0:1], car_im[:, j:j + 1], mag[:, j:j + 1], c2[:, 0:1], ALU.mult, ALU.add, reads=[car_b[j], mag.b, c2.b], writes=[s0.b])
                CP(P, "dve", s0[:, 0, 1:T], a[:, 1:T], reads=[a.b], writes=[s0.b])
                CP(P, "dve", s0[:, 1, 1:T], c2[:, 1:T], reads=[c2.b], writes=[s0.b])
                cur, nxt = s0, s1_
                for k in range(NST):
                    w_ = 1 << k
                    CP(P, "pool", nxt[:, :, 0:w_], cur[:, :, 0:w_], reads=[cur.b], writes=[nxt.b])
                    STT(P, "dve", nxt[:, :, w_:T], cur[:, :, 0:T - w_], RP[:, k, j:j + 1], cur[:, :, w_:T], ALU.mult, ALU.add,
                        reads=[cur.b, RP.b], writes=[nxt.b])
                    cur, nxt = nxt, cur
                gr, gi = cur[:, 0, :], cur[:, 1, :]
                gr_b = gi_b = cur.b
            ct, ct1 = ctmp.next(), ctmp.next()
            TS(P, "dve", ct[:], gi[:, T - 1:T], SIN[:, j, T - 1:T], None, ALU.mult, None, reads=[gi_b, SIN.b], writes=[ct.b])
            TS(P, "dve", ct1[:], gr[:, T - 1:T], COS[:, j, T - 1:T], None, ALU.mult, None, reads=[gr_b, COS.b], writes=[ct1.b])
            ct2, ct3 = ctmp.next(), ctmp.next()
            TS(P, "dve", ct2[:], gi[:, T - 1:T], COS[:, j, T - 1:T], None, ALU.mult, None, reads=[gi_b, COS.b], writes=[ct2.b])
            TS(P, "dve", ct3[:], gr[:, T - 1:T], SIN[:, j, T - 1:T], None, ALU.mult, None, reads=[gr_b, SIN.b], writes=[ct3.b])
            TT(P, "dve", car_re[:, j:j + 1], ct1[:], ct[:], ALU.subtract, reads=[ct1.b, ct.b], writes=[car_b[j]])
            TT(P, "dve", car_im[:, j:j + 1], ct3[:], ct2[:], ALU.add, reads=[ct3.b, ct2.b], writes=[car_b[j]])
            TT(P, "dve", a[:], gr, COS[:, j, :], ALU.mult, reads=[gr_b, COS.b], writes=[a.b])
            TT(P, "dve", b[:], gi, SIN[:, j, :], ALU.mult, reads=[gi_b, SIN.b], writes=[b.b])
            TT(P, "dve", c2[:], gr, SIN[:, j, :], ALU.mult, reads=[gr_b, SIN.b], writes=[c2.b])
            TT(P, "dve", d[:], gi, COS[:, j, :], ALU.mult, reads=[gi_b, COS.b], writes=[d.b])
            TT(P, "dve", H_[:, 2 * j, :], a[:], b[:], ALU.subtract, reads=[a.b, b.b], writes=[H_.b])
            TT(P, "dve", H_[:, 2 * j + 1, :], c2[:], d[:], ALU.add, reads=[c2.b, d.b], writes=[H_.b])
            yield
        y_ = ysb.next()
        for cc in range(2):
            py = ps_y.next()
            for k in range(8):
                MM(P, py[:, :], CT[:, 8 * cc + k, :], H_[:, 8 * cc + k, :], k == 0, k == 7, reads=[CT.b, H_.b], writes=[py.b])
            STT(P, "dve", y_[:, cc, :], u_[:, cc, :], dsk[:, cc:cc + 1], py[:, :], ALU.mult, ALU.add, reads=[u_.b, dsk.b, py.b], writes=[y_.b])
        a1, a2, yg_, ygb_ = g1.next(), g2.next(), yg.next(), ygb.next()
        emit_gelu(C, yg_[:], y_[:], None, a1[:], a2[:], [y_.b], [yg_.b], None, a1.b, a2.b)
        CP(P, "act", ygb_[:], yg_[:], reads=[yg_.b], writes=[ygb_.b])
        yc_ = yc.next()
        for co in range(2):
            pg = ps_y.next()
            for k in range(2):
                MM(P, pg[:, :], Wglu[:, k, co * 128:(co + 1) * 128], ygb_[:, k, :], k == 0, k == 1, reads=[Wglu.b, ygb_.b], writes=[pg.b])
            sg = sig.next()
            TS(P, "dve", sg[:], pg[:, :], bglu[:, co:co + 1], None, ALU.add, None, reads=[pg.b, bglu.b], writes=[sg.b])
            TS(P, "dve", sg[:], sg[:], -43.0, None, ALU.max, None, reads=[sg.b], writes=[sg.b])
            emit_sigmoid(C, sg[:], sg[:], [sg.b], [sg.b])
            TT(P, "dve", yc_[:, co, :], yg_[:, co, :], sg[:], ALU.mult, reads=[yg_.b, sg.b], writes=[yc_.b])
        sq_ = ysq.next()
        TT(P, "dve", sq_[:], yc_[:], yc_[:], ALU.mult, reads=[yc_.b], writes=[sq_.b])
        pr = ps_y.next()
        for k in range(2):
            MM(P, pr[:, :], on256[:], sq_[:, k, :], k == 0, k == 1, reads=[on256.b, sq_.b], writes=[pr.b])
        r_ = rr_.next()
        emit_rsqrt(C, r_[:], pr[:, :], 1.0, C.eps_rms, [pr.b], [r_.b])
        yn = ycn.next()
        for cc in range(2):
            TT(P, "dve", yn[:, cc, :], yc_[:, cc, :], r_[:], ALU.mult, reads=[yc_.b, r_.b], writes=[yn.b])
        P.dma("sp", [(YT[768:1024, cs].rearrange("(k p) t -> p k t", p=128), yn[:])], reads=[yn.b])
        yield


def phase_s5(C, UT, YT, W, l, maskC):
    ph = Phase(C)
    for _ in gen_s5(C, ph, UT, YT, W, l, maskC, C.ps[0:4], C.ps[4:8], C.ps[0:3], 2):
        pass
    ph.close()


def phase_attn_s5(C, QT, KT, VA, UT, YT, W, l, triu, shiftm, maskC):
    ph = Phase(C)
    gs = gen_s5(C, ph, UT, YT, W, l, maskC, C.ps[5:7], C.ps[7:8], C.ps[0:3], 1)
    ga = gen_attn(C, ph, QT, KT, VA, YT, triu, shiftm, C.ps[0:2], C.ps[2:4], C.ps[4:5], 1)
    S = C.S
    n_a = NH * sum(4 * (j + 1) for j in range(S // 512)) + 2
    n_s = (S // 512) * 9
    done_a = done_s = False
    ia = is_ = 0
    while not (done_a and done_s):
        if not done_s and (done_a or is_ * n_a <= ia * n_s):
            try:
                next(gs)
                is_ += 1
            except StopIteration:
                done_s = True
            continue
        try:
            next(ga)
            ia += 1
        except StopIteration:
            done_a = True
    ph.close()


def phase_mix(C, x_in, x_out, YT, W, l):
    P, S, nc = C.P, C.S, C.nc
    G = 512
    NG = S // G
    ph = Phase(C)
    Wo = ph.tile("Wo", [128, 8, D], BF16)
    stg = ph.ring("stgo", 2, [128, D], F32)
    gm = ph.tile("gm", [128, 8], F32)
    with nc.allow_non_contiguous_dma(reason="tiny gain vector"):
        P.dma("sp", [(gm[:], W["mix_norm_g"][l].rearrange("(k p) -> p k", p=128))], writes=[gm.b])
    wo_v = W["w_out"][l].rearrange("(k p) d -> p k d", p=128)
    for k in range(8):
        s_ = stg.next()
        P.dma("sp", [(s_[:], wo_v[:, k, :])], writes=[s_.b])
        TS(P, "dve", Wo[:, k, :], s_[:], gm[:, k:k + 1], None, ALU.mult, None, reads=[s_.b, gm.b], writes=[Wo.b])
    Gbc = ph.tile("Gbc", [128, D], F32)
    Bbc = ph.tile("Bbc", [128, D], F32)
    P.dma("sp", [(Gbc[:], W["ln_g"][l, 1].rearrange("(o d) -> o d", o=1).broadcast_to([128, D]))], writes=[Gbc.b])
    P.dma("sp", [(Bbc[:], W["ln_b"][l, 1].rearrange("(o d) -> o d", o=1).broadcast_to([128, D]))], writes=[Bbc.b])
    on512 = ph.tile("on512", [128, 128], BF16)
    P.op("dve", lambda e: e.memset(on512[:], 1.0 / 512), writes=[on512.b])
    Yt = ph.ring("Yt", 2, [128, 8, G], BF16)
    ysq = ph.ring("ysq", 2, [128, 4, G], BF16)
    rb = ph.ring("rb", 2, [128, G], F32)
    XR = ph.ring("XR", 3, [128, D], F32)
    ZR = ph.ring("ZR", 2, [128, D], F32)
    st6 = ph.ring("st6", 2, [128, 2, 6], F32)
    mv = ph.ring("mv", 2, [128, 2], F32)
    rstd = ph.ring("rstd", 2, [128, 1], F32)
    ps_r = PsRing(C.ps[0:2])
    ps_o = PsRing(C.ps[2:8])
    for g in range(NG):
        cs = slice(g * G, (g + 1) * G)
        Y = Yt.next()
        P.dma("sp", [(Y[:], YT[:, cs].rearrange("(k p) t -> p k t", p=128))], writes=[Y.b])
        sq_ = ysq.next()
        TT(P, "dve", sq_[:], Y[:, 2:6, :], Y[:, 2:6, :], ALU.mult, reads=[Y.b], writes=[sq_.b])
        pr = ps_r.next()
        for k in range(4):
            MM(P, pr[:, :], on512[:], sq_[:, k, :], k == 0, k == 3, reads=[on512.b, sq_.b], writes=[pr.b])
        r_ = rb.next()
        emit_rsqrt(C, r_[:], pr[:, :], 1.0, C.eps_rms, [pr.b], [r_.b])
        for k in range(2, 6):
            TT(P, "dve", Y[:, k, :], Y[:, k, :], r_[:], ALU.mult, reads=[Y.b, r_.b], writes=[Y.b])
        for s in range(4):
            r0 = g * G + s * 128
            X = XR.next()
            P.dma("sp", [(X[:], x_in[r0:r0 + 128, :])], writes=[X.b])
            Z = ZR.next()
            for dh in range(2):
                po = ps_o.next()
                for k in range(8):
                    MM(P, po[:, :], Y[:, k, s * 128:(s + 1) * 128], Wo[:, k, dh * 512:(dh + 1) * 512], k == 0, k == 7, reads=[Y.b, Wo.b], writes=[po.b])
                CP(P, "act", Z[:, dh * 512:(dh + 1) * 512], po[:, :], reads=[po.b], writes=[Z.b])
            STT(P, "dve", Z[:], X[:], ALPHA, Z[:], ALU.mult, ALU.add, reads=[X.b, Z.b], writes=[Z.b])
            emit_ln_epilogue(C, Z, mv.next(), st6.next(), rstd.next(), Gbc, Bbc)
            P.dma("sp", [(x_out[r0:r0 + 128, :], Z[:])], reads=[Z.b], owner=Z.b)
    ph.close()


WEIGHT_NAMES = ["ln_g", "ln_b", "ffn1_w_gate", "ffn1_w_up", "ffn1_w_down", "w_in", "gmlp_norm_g", "gmlp_ws", "gmlp_bs",
                "mla_q_norm_g", "mla_w_uq", "mla_kv_norm_g", "mla_w_ukv", "ssm_a_re", "ssm_a_im", "ssm_b_re", "ssm_b_im",
                "ssm_c_re", "ssm_c_im", "ssm_d", "ssm_log_dt", "ssm_glu_w", "ssm_glu_b", "mix_norm_g", "w_out",
                "ffn2_w_gate", "ffn2_w_up", "ffn2_w_down"]


def host_consts():
    inv = (1.0 / (10000.0 ** (np.arange(0, 32, 2, dtype=np.float32) / 32.0))).astype(np.float32)
    shiftm = np.zeros((128, 128), np.float32)
    shiftm[np.arange(64) + 64, np.arange(64)] = 1
    maskC = np.zeros((128, 4, 128), np.float32)
    for j4 in range(4):
        for n in range(128):
            c0 = j4 * 32 + (n // 64) * 16
            maskC[n, j4, c0:c0 + 16] = 1
    return {
        "c_ident": np.eye(128, dtype=np.float32),
        "c_tril": np.tril(np.ones((128, 128), np.float32)),
        "c_triu": np.triu(np.ones((128, 128), np.float32)),
        "c_shift": shiftm,
        "c_maskC": maskC,
        "c_invf": np.concatenate([inv, inv])[:, None].astype(np.float32),
    }


def build_program(S, wshapes, depth=DEPTH, phases="ABCDEF"):
    nc = bass.Bass("TRN2", target_bir_lowering=False)
    x = nc.dram_tensor("x", [S, D], F32, kind="ExternalInput").ap()
    pos = nc.dram_tensor("positions", [S], I32, kind="ExternalInput").ap()
    W = {k: nc.dram_tensor(k, list(wshapes[k]), F32, kind="ExternalInput").ap() for k in WEIGHT_NAMES}
    hc = host_consts()
    Cn = {k: nc.dram_tensor(k, list(v.shape), F32, kind="ExternalInput").ap() for k, v in hc.items()}
    y = nc.dram_tensor("y", [S, D], F32, kind="ExternalOutput").ap()
    XS = nc.dram_tensor("XS", [S, D], F32).ap()
    ROT = nc.dram_tensor("ROT", [2, 32, S], F32).ap()
    YT = nc.dram_tensor("YT", [1024, S], BF16).ap()
    QT = nc.dram_tensor("QT", [NH, 96, S], BF16).ap()
    KT = nc.dram_tensor("KT", [NH, 96, S], BF16).ap()
    VA = nc.dram_tensor("VA", [S, 512], BF16).ap()
    UT = nc.dram_tensor("UT", [256, S], F32).ap()
    with ExitStack() as st:
        C = Ctx(nc, st, S)
        st.enter_context(nc.Block())
        C.load_consts(Cn["c_ident"])
        phase_rope_tables(C, pos, Cn["c_invf"], ROT)
        for l in range(depth):
            src = x if l == 0 else XS
            phase_ffn(C, src, XS, "xa", "xb", W["ffn1_w_gate"][l], W["ffn1_w_up"][l], W["ffn1_w_down"][l], W["ln_g"][l, 0], W["ln_b"][l, 0])
            phase_proj(C, XS, W, l, ROT, YT, QT, KT, VA, UT, Cn["c_tril"])
            if INTERLEAVE_CD:
                phase_attn_s5(C, QT, KT, VA, UT, YT, W, l, Cn["c_triu"], Cn["c_shift"], Cn["c_maskC"])
            else:
                phase_attn(C, QT, KT, VA, YT, Cn["c_triu"], Cn["c_shift"])
                phase_s5(C, UT, YT, W, l, Cn["c_maskC"])
            phase_mix(C, XS, XS, YT, W, l)
            dst = y if l == depth - 1 else XS
            phase_ffn(C, XS, dst, "xc", "xd", W["ffn2_w_gate"][l], W["ffn2_w_up"][l], W["ffn2_w_down"][l], W["ln_g"][l, 2], W["ln_b"][l, 2])
        C.P.barrier()
        C.ninst = C.P.ninst
    return nc


_CACHE = {}


def kernel(**inputs):
    x = np.ascontiguousarray(np.asarray(inputs["x"], dtype=np.float32))
    B, S, _ = x.shape
    pos = np.ascontiguousarray(np.asarray(inputs["positions"], dtype=np.int32))
    wts = {k: np.ascontiguousarray(np.asarray(inputs[k], dtype=np.float32)) for k in WEIGHT_NAMES}
    key = (S,)
    if key not in _CACHE:
        _CACHE[key] = build_program(S, {k: v.shape for k, v in wts.items()})
    nc = _CACHE[key]
    hc = host_consts()
    in_maps = []
    for b in range(B):
        m = {"x": x[b], "positions": pos[b]}
        m.update(wts)
        m.update(hc)
        in_maps.append(m)
    res = run_bass_kernel_spmd(nc, in_maps, core_ids=list(range(B)))
    return np.stack([np.asarray(r["y"], dtype=np.float32) for r in res.results], axis=0)
```
